# Optimizing a Trainium2 kernel written in Bass

```python
import jax, jax.numpy as jnp
from jax import lax
import numpy as np

D_MODEL = 1024
BATCH = 32
SEQ = 256
DEPTH = 2
DEC_BATCH = 2
DEC_SEQ = 4096
PAST_LEN = 256

GRID_W = 64
N_MIXERS = 2
N_ATTN_LAYERS = (DEPTH + N_MIXERS - 1) // N_MIXERS
N_CONV_LAYERS = DEPTH // N_MIXERS
HEAD_DIM = 64
N_HEADS = D_MODEL // HEAD_DIM
N_KV_HEADS = 4
GROUP = N_HEADS // N_KV_HEADS
QKV_WIDTH = (N_HEADS + 2 * N_KV_HEADS) * HEAD_DIM
WINDOW = 128
BLOCK = 128
ROPE_THETA = 10000.0
CONV_WIDTH = 31
CONV_PAD = (CONV_WIDTH - 1) // 2
D_FF = 4 * D_MODEL
N_MOD = 6
DN_ALPHA = (2.0 * DEPTH) ** 0.25
DN_BETA = (8.0 * DEPTH) ** -0.25
LN_EPS = 1e-5
NEG_INF = -1e30
ATTN_SCALE = HEAD_DIM ** -0.5

kernel_name = "hybrid_diffusion_swa_conformer_step"


def layer_norm(x, g, b):
    xf = x.astype(jnp.float32)
    mu = jnp.mean(xf, axis=-1, keepdims=True)
    var = jnp.mean(jnp.square(xf - mu), axis=-1, keepdims=True)
    y = (xf - mu) * lax.rsqrt(var + LN_EPS)
    return (y * g + b).astype(x.dtype)


def modulation(cond, w, b):
    m = (jax.nn.silu(cond) @ w + b)[..., None, :]
    return jnp.split(m, N_MOD, axis=-1)


def modulate(x, shift, scale):
    return x * (1 + scale) + shift


def post_norm_update(x, sub_out, gate, g, b):
    return layer_norm(DN_ALPHA * x + gate * sub_out, g, b)


def axial_rope_angles(n_tokens):
    rows = n_tokens // GRID_W
    row = jnp.repeat(jnp.arange(rows, dtype=jnp.float32), GRID_W)
    col = jnp.tile(jnp.arange(GRID_W, dtype=jnp.float32), rows)
    half = HEAD_DIM // 2
    inv = ROPE_THETA ** (-jnp.arange(0, half, 2, dtype=jnp.float32) / half)
    return row[:, None] * inv, col[:, None] * inv


def _rotate(x, ang):
    cos = jnp.cos(ang)[:, None, :]
    sin = jnp.sin(ang)[:, None, :]
    d2 = x.shape[-1] // 2
    x1, x2 = x[..., :d2], x[..., d2:]
    return jnp.concatenate([x1 * cos - x2 * sin, x2 * cos + x1 * sin], axis=-1).astype(x.dtype)


def apply_axial_rope(x, ang_r, ang_c):
    half = HEAD_DIM // 2
    return jnp.concatenate([_rotate(x[..., :half], ang_r), _rotate(x[..., half:], ang_c)], axis=-1)


def qkv_project(h, w_qkv):
    B, L, _ = h.shape
    q, k, v = jnp.split(h @ w_qkv, [N_HEADS * HEAD_DIM, (N_HEADS + N_KV_HEADS) * HEAD_DIM], axis=-1)
    return (q.reshape(B, L, N_HEADS, HEAD_DIM),
            k.reshape(B, L, N_KV_HEADS, HEAD_DIM),
            v.reshape(B, L, N_KV_HEADS, HEAD_DIM))


def _sink_column(sink, shape):
    s = sink.astype(jnp.float32).reshape(N_KV_HEADS, GROUP, 1, 1)
    return jnp.broadcast_to(s, shape[:-1] + (1,))


def context_attention(q, k, v, sink):
    B, L = q.shape[:2]
    qg = q.reshape(B, L, N_KV_HEADS, GROUP, HEAD_DIM)
    s = jnp.einsum('bqhgd,bshd->bhgqs', qg, k, preferred_element_type=jnp.float32) * ATTN_SCALE
    p = jax.nn.softmax(jnp.concatenate([s, _sink_column(sink, s.shape)], axis=-1), axis=-1)[..., :-1]
    o = jnp.einsum('bhgqs,bshd->bqhgd', p.astype(v.dtype), v)
    return o.reshape(B, L, D_MODEL)


def latent_window_attention(q, k, v, k_ctx, v_ctx, sink):
    B, L = q.shape[:2]
    nb = L // BLOCK
    qb = q.reshape(B, nb, BLOCK, N_KV_HEADS, GROUP, HEAD_DIM)
    pad = ((0, 0), (BLOCK, BLOCK), (0, 0), (0, 0))
    kr = jnp.pad(k, pad).reshape(B, nb + 2, BLOCK, N_KV_HEADS, HEAD_DIM)
    vr = jnp.pad(v, pad).reshape(B, nb + 2, BLOCK, N_KV_HEADS, HEAD_DIM)
    kb = jnp.concatenate([kr[:, :-2], kr[:, 1:-1], kr[:, 2:]], axis=2)
    vb = jnp.concatenate([vr[:, :-2], vr[:, 1:-1], vr[:, 2:]], axis=2)
    band = 3 * BLOCK
    qi = jnp.arange(BLOCK)[:, None]
    si = jnp.arange(band)[None, :]
    in_window = jnp.abs(qi - si + BLOCK) <= WINDOW
    kpos = (jnp.arange(nb)[:, None] - 1) * BLOCK + si
    in_range = (kpos >= 0) & (kpos < L)
    mask = in_window[None] & in_range[:, None, :]
    s_loc = jnp.einsum('bnqhgd,bnshd->bnhgqs', qb, kb, preferred_element_type=jnp.float32) * ATTN_SCALE
    s_loc = jnp.where(mask[None, :, None, None], s_loc, NEG_INF)
    s_ctx = jnp.einsum('bnqhgd,bchd->bnhgqc', qb, k_ctx, preferred_element_type=jnp.float32) * ATTN_SCALE
    logits = jnp.concatenate([s_loc, s_ctx, _sink_column(sink, s_loc.shape)], axis=-1)
    p = jax.nn.softmax(logits, axis=-1).astype(v.dtype)
    n_ctx = k_ctx.shape[1]
    o = (jnp.einsum('bnhgqs,bnshd->bnqhgd', p[..., :band], vb)
         + jnp.einsum('bnhgqc,bchd->bnqhgd', p[..., band:band + n_ctx], v_ctx))
    return o.reshape(B, L, D_MODEL)


def conformer_conv(h, pw1_w, pw1_b, dw_w, dw_b, cn_g, cn_b, pw2_w, pw2_b):
    a, gte = jnp.split(h @ pw1_w + pw1_b, 2, axis=-1)
    u = a * jax.nn.sigmoid(gte)
    u = lax.conv_general_dilated(u, dw_w[:, None, :].astype(u.dtype), window_strides=(1,),
                                 padding=[(CONV_PAD, CONV_PAD)],
                                 dimension_numbers=('NWC', 'WIO', 'NWC'),
                                 feature_group_count=D_MODEL) + dw_b
    u = jax.nn.silu(layer_norm(u, cn_g, cn_b))
    return u @ pw2_w + pw2_b


def sq_relu_mlp(h, w1, b1, w2, b2):
    return jnp.square(jax.nn.relu(h @ w1 + b1)) @ w2 + b2


def setup_inputs(seed: int = 0) -> dict:
    key = jax.random.key(seed)
    ks = jax.random.split(key, 32)
    f32 = jnp.float32
    D = D_MODEL
    nrm = lambda k, shape, s: jax.random.normal(k, shape, f32) * s
    w_q_k = nrm(ks[0], (N_ATTN_LAYERS, D, (N_HEADS + N_KV_HEADS) * HEAD_DIM), D ** -0.5)
    w_v = nrm(ks[1], (N_ATTN_LAYERS, D, N_KV_HEADS * HEAD_DIM), DN_BETA * D ** -0.5)
    return {
        "x_prompt": nrm(ks[2], (BATCH, SEQ, D), 1.0),
        "x_sample": nrm(ks[3], (DEC_BATCH, DEC_SEQ, D), 1.0),
        "cache_k": nrm(ks[4], (DEC_BATCH, N_ATTN_LAYERS, PAST_LEN, N_KV_HEADS, HEAD_DIM), 1.0),
        "cache_v": nrm(ks[5], (DEC_BATCH, N_ATTN_LAYERS, PAST_LEN, N_KV_HEADS, HEAD_DIM), DN_BETA),
        "c": nrm(ks[6], (DEC_BATCH, D), 1.0),
        "c_ctx": nrm(ks[7], (D,), 1.0),
        "ada_w": nrm(ks[8], (DEPTH, D, N_MOD * D), 0.5 * D ** -0.5),
        "ada_b": nrm(ks[9], (DEPTH, N_MOD * D), 0.02),
        "attn_w_qkv": jnp.concatenate([w_q_k, w_v], axis=-1),
        "attn_w_o": nrm(ks[10], (N_ATTN_LAYERS, D, D), DN_BETA * D ** -0.5),
        "attn_sink": nrm(ks[11], (N_ATTN_LAYERS, N_HEADS), 0.5),
        "conv_pw1_w": nrm(ks[12], (N_CONV_LAYERS, D, 2 * D), D ** -0.5),
        "conv_pw1_b": nrm(ks[13], (N_CONV_LAYERS, 2 * D), 0.02),
        "conv_dw_w": nrm(ks[14], (N_CONV_LAYERS, CONV_WIDTH, D), CONV_WIDTH ** -0.5),
        "conv_dw_b": nrm(ks[15], (N_CONV_LAYERS, D), 0.02),
        "conv_norm_g": 1.0 + nrm(ks[16], (N_CONV_LAYERS, D), 0.02),
        "conv_norm_b": nrm(ks[17], (N_CONV_LAYERS, D), 0.02),
        "conv_pw2_w": nrm(ks[18], (N_CONV_LAYERS, D, D), DN_BETA * D ** -0.5),
        "conv_pw2_b": nrm(ks[19], (N_CONV_LAYERS, D), 0.02),
        "ln1_g": 1.0 + nrm(ks[20], (DEPTH, D), 0.02),
        "ln1_b": nrm(ks[21], (DEPTH, D), 0.02),
        "mlp_w1": nrm(ks[22], (DEPTH, D, D_FF), D ** -0.5),
        "mlp_b1": nrm(ks[23], (DEPTH, D_FF), 0.02),
        "mlp_w2": nrm(ks[24], (DEPTH, D_FF, D), DN_BETA * D_FF ** -0.5),
        "mlp_b2": nrm(ks[25], (DEPTH, D), 0.02),
        "ln2_g": 1.0 + nrm(ks[26], (DEPTH, D), 0.02),
        "ln2_b": nrm(ks[27], (DEPTH, D), 0.02),
    }


def reference(x_prompt, x_sample, cache_k, cache_v, c, c_ctx, ada_w, ada_b,
              attn_w_qkv, attn_w_o, attn_sink,
              conv_pw1_w, conv_pw1_b, conv_dw_w, conv_dw_b, conv_norm_g, conv_norm_b,
              conv_pw2_w, conv_pw2_b,
              ln1_g, ln1_b, mlp_w1, mlp_b1, mlp_w2, mlp_b2, ln2_g, ln2_b):
    ang_r, ang_c = axial_rope_angles(x_sample.shape[1])
    yp, ys = x_prompt, x_sample
    new_k, new_v = [], []
    for i in range(DEPTH):
        sh1p, sc1p, g1p, sh2p, sc2p, g2p = modulation(c_ctx, ada_w[i], ada_b[i])
        sh1s, sc1s, g1s, sh2s, sc2s, g2s = modulation(c, ada_w[i], ada_b[i])
        hp = modulate(yp, sh1p, sc1p)
        hs = modulate(ys, sh1s, sc1s)
        j = i // N_MIXERS
        if i % N_MIXERS == 0:
            qp, kp, vp = qkv_project(hp, attn_w_qkv[j])
            new_k.append(kp)
            new_v.append(vp)
            op = context_attention(qp, kp, vp, attn_sink[j]) @ attn_w_o[j]
            qs, ks_, vs = qkv_project(hs, attn_w_qkv[j])
            qs = apply_axial_rope(qs, ang_r, ang_c)
            ks_ = apply_axial_rope(ks_, ang_r, ang_c)
            os_ = latent_window_attention(qs, ks_, vs, cache_k[:, j], cache_v[:, j], attn_sink[j]) @ attn_w_o[j]
        else:
            cw = (conv_pw1_w[j], conv_pw1_b[j], conv_dw_w[j], conv_dw_b[j],
                  conv_norm_g[j], conv_norm_b[j], conv_pw2_w[j], conv_pw2_b[j])
            op = conformer_conv(hp, *cw)
            os_ = conformer_conv(hs, *cw)
        yp = post_norm_update(yp, op, g1p, ln1_g[i], ln1_b[i])
        ys = post_norm_update(ys, os_, g1s, ln1_g[i], ln1_b[i])
        mw = (mlp_w1[i], mlp_b1[i], mlp_w2[i], mlp_b2[i])
        yp = post_norm_update(yp, sq_relu_mlp(modulate(yp, sh2p, sc2p), *mw), g2p, ln2_g[i], ln2_b[i])
        ys = post_norm_update(ys, sq_relu_mlp(modulate(ys, sh2s, sc2s), *mw), g2s, ln2_g[i], ln2_b[i])
    new_cache_k = jnp.stack(new_k, axis=1)
    new_cache_v = jnp.stack(new_v, axis=1)
    return (yp, ys, new_cache_k, new_cache_v)
```

```python
import numpy as np
import concourse.bass as bass
import concourse.mybir as mybir
from concourse.bass_utils import run_bass_kernel_spmd

F32 = mybir.dt.float32
BF16 = mybir.dt.bfloat16
AF = mybir.ActivationFunctionType
ALU = mybir.AluOpType
DTSZ = {F32: 4, BF16: 2}
CELL = 32

D = 1024
NCH = 8
ALPHA = 4.0 ** 0.25
EPS = 1e-5
SCALE = 0.125


class V:
    def __init__(self, space, lo, hi, ap):
        self.space, self.lo, self.hi, self.ap = space, lo, hi, ap


class T(V):
    def __init__(self, space, lo, hi, ap, shape, esz):
        super().__init__(space, lo, hi, ap)
        self.shape, self.esz = shape, esz

    def v(self, c, a0, w):
        n = self.shape[-1]
        lo = self.lo + (c * n + a0) * self.esz
        return V(self.space, lo, lo + w * self.esz, self.ap[:, c, a0:a0 + w])

    def vs(self, a0, w):
        return [self.v(c, a0, w) for c in range(self.shape[0])]

    def cols(self, a0, w):
        lo = self.lo + a0 * self.esz
        return V(self.space, lo, lo + w * self.esz, self.ap[:, a0:a0 + w])


class _Rec:
    def __init__(self):
        self.call = None

    def __getattr__(self, name):
        def f(*a, **k):
            self.call = (name, a, k)
        return f


def _eager(fn):
    r = _Rec()
    fn(r)
    name, a, k = r.call
    return lambda h: getattr(h, name)(*a, **k)


class Stream:
    def __init__(self, name, sem, idx):
        self.name, self.sem, self.idx, self.val = name, sem, idx, 0


class Engine:
    def __init__(self, name, stream, self_sync):
        self.name, self.stream, self.self_sync = name, stream, self_sync
        self.ops, self.seen, self.pending_noinc = [], {}, False


class Sched:
    def __init__(self, nc, sbuf_bytes, n_dma_slots=10):
        self.nc, self._cm, self.streams, self.eng = nc, [], [], {}
        for name, ss in (("pe", False), ("act", True), ("dve", True), ("pool", True), ("sp", False)):
            self.eng[name] = Engine(name, self._new_stream(name), ss)
        self.dma_slots = {"hw": [self._new_stream(f"dma{i}") for i in range(n_dma_slots)],
                          "sw": [self._new_stream(f"swdma{i}") for i in range(n_dma_slots)]}
        self.dma_rr = {"hw": 0, "sw": 0}
        ns = len(self.streams)
        self.sbuf_bytes = sbuf_bytes
        self.ncell = {"sb": sbuf_bytes // CELL + 1, "ps": 16384 // CELL}
        self.wv = {sp: np.zeros((ns, n), np.int64) for sp, n in self.ncell.items()}
        self.rv = {sp: np.zeros((ns, n), np.int64) for sp, n in self.ncell.items()}
        self.sb = self._enter(nc.sbuf_tensor("arena", [128, sbuf_bytes // 4], F32))
        self.ps = self._enter(nc.psum_tensor("psarena", [128, 4096], F32))
        self.sb_ptr = 0

    def _enter(self, cm):
        self._cm.append(cm)
        return cm.__enter__()

    def _new_stream(self, name):
        st = Stream(name, self._enter(self.nc.semaphore(name)), len(self.streams))
        self.streams.append(st)
        return st

    def close(self):
        for cm in reversed(self._cm):
            cm.__exit__(None, None, None)

    def alloc(self, shape, dtype, at=None):
        cnt = int(np.prod(shape))
        n = cnt * DTSZ[dtype]
        if at is None:
            at = self.sb_ptr
            self.sb_ptr = (at + n + 63) // 64 * 64
        assert at % 4 == 0 and at + n <= self.sbuf_bytes, (at, n, self.sbuf_bytes)
        ap = self.sb[:, at // 4:(at + n + 3) // 4]
        if dtype != F32:
            ap = ap.bitcast(dtype)[:, 0:cnt]
        if len(shape) > 1:
            names = " ".join(f"d{i}" for i in range(len(shape)))
            ap = ap.rearrange(f"p ({names}) -> p {names}", **{f"d{i}": s for i, s in enumerate(shape)})
        return T("sb", at, at + n, ap, list(shape), DTSZ[dtype])

    def psum(self, bank, width=512):
        lo = bank * 2048
        return T("ps", lo, lo + width * 4, self.ps[:, bank * 512: bank * 512 + width], [width], 4)

    @staticmethod
    def _cells(v):
        if v.space == "ps":
            return (v.lo // 2048) * (2048 // CELL), ((v.hi - 1) // 2048 + 1) * (2048 // CELL)
        return v.lo // CELL, (v.hi - 1) // CELL + 1

    def _deps(self, reads, writes, own=None):
        deps = {}
        for v in reads:
            a, b = self._cells(v)
            m = self.wv[v.space][:, a:b].max(axis=1)
            if v.space == "ps":
                m2 = self.rv[v.space][:, a:b].max(axis=1)
                if own is not None:
                    m2[own] = 0
                m = np.maximum(m, m2)
            for i in np.nonzero(m)[0]:
                deps[i] = max(deps.get(i, 0), int(m[i]))
        for v in writes:
            a, b = self._cells(v)
            m = np.maximum(self.wv[v.space][:, a:b].max(axis=1), self.rv[v.space][:, a:b].max(axis=1))
            for i in np.nonzero(m)[0]:
                deps[i] = max(deps.get(i, 0), int(m[i]))
        return deps

    def _record(self, st, val, reads, writes):
        for v in reads:
            a, b = self._cells(v)
            self.rv[v.space][st.idx, a:b] = val
        for v in writes:
            a, b = self._cells(v)
            self.wv[v.space][:, a:b] = 0
            self.rv[v.space][:, a:b] = 0
            self.wv[v.space][st.idx, a:b] = val

    def _waits(self, e, deps):
        waits = []
        for i, val in deps.items():
            st = self.streams[i]
            if st is e.stream and not e.self_sync:
                continue
            if e.seen.get(i, 0) >= val:
                continue
            e.seen[i] = val
            waits.append((st.sem, val))
        return waits

    @staticmethod
    def _flat(lst):
        out = []
        for r in lst:
            if r is None:
                continue
            if isinstance(r, (list, tuple)):
                out.extend(x for x in r if x is not None)
            else:
                out.append(r)
        return out

    def op(self, eng, fn, reads=(), writes=(), inc=True):
        e = self.eng[eng]
        reads, writes = self._flat(reads), self._flat(writes)
        waits = self._waits(e, self._deps(reads, writes, own=e.stream.idx))
        st = e.stream
        val = st.val + 1
        if inc:
            st.val = val
        e.pending_noinc = not inc
        e.ops.append((waits, _eager(fn), (st.sem, 1) if inc else None))
        self._record(st, val, reads, writes)

    def dma(self, queue, out, in_, reads=(), writes=()):
        e = self.eng[queue]
        reads, writes = self._flat(reads), self._flat(writes)
        kind = "sw" if queue == "pool" else "hw"
        slot = self.dma_slots[kind][self.dma_rr[kind]]
        self.dma_rr[kind] = (self.dma_rr[kind] + 1) % len(self.dma_slots[kind])
        deps = self._deps(reads, writes)
        if slot.val:
            deps[slot.idx] = max(deps.get(slot.idx, 0), slot.val)
        waits = self._waits(e, deps)
        slot.val += 16
        e.ops.append((waits, lambda h: h.dma_start(out=out, in_=in_), (slot.sem, 16)))
        self._record(slot, slot.val, reads, writes)

    def finish(self, queue="sp"):
        e = self.eng[queue]
        waits = [(st.sem, st.val) for st in self.streams
                 if st.val and e.seen.get(st.idx, 0) < st.val and st is not e.stream]
        e.ops.append((waits, None, None))

    def emit(self):
        for e in self.eng.values():
            assert not e.pending_noinc, e.name
        with self.nc.Block() as block:
            def run(e):
                def body(h):
                    for waits, fn, inc in e.ops:
                        for sem, val in waits:
                            h.wait_ge(sem, val)
                        if fn is not None:
                            ins = fn(h)
                            if inc is not None:
                                ins.then_inc(inc[0], inc[1])
                return body
            block.tensor(run(self.eng["pe"]))
            block.scalar(run(self.eng["act"]))
            block.vector(run(self.eng["dve"]))
            block.gpsimd(run(self.eng["pool"]))
            block.sync(run(self.eng["sp"]))


VEC_SPECS = [("ada_b0", 48), ("ada_b1", 48), ("ln1_g0", 8), ("ln1_b0", 8), ("ln2_g0", 8), ("ln2_b0", 8),
             ("ln1_g1", 8), ("ln1_b1", 8), ("ln2_g1", 8), ("ln2_b1", 8), ("b1_0", 32), ("b1_1", 32),
             ("b2_0", 8), ("b2_1", 8), ("pw1_b", 16), ("dw_w", 248), ("dw_b", 8), ("cn_g", 8), ("cn_b", 8),
             ("pw2_b", 8), ("es", 16), ("valid", 2), ("zero", 8)]
VOFF = {}
_o = 0
for _n, _c in VEC_SPECS:
    VOFF[_n] = _o
    _o += _c
NVEC = _o

PIECES = ([f"ada0_{j}" for j in range(6)] + [f"ada1_{j}" for j in range(6)] +
          ["wq", "wqs", "wk", "wv", "wo"] + [f"w1_0_{f}" for f in range(4)] + [f"w2_0_{f}" for f in range(4)] +
          ["pw1a", "pw1g", "pw2"] + [f"w1_1_{f}" for f in range(4)] + [f"w2_1_{f}" for f in range(4)])
PIDX = {n: i for i, n in enumerate(PIECES)}


def _mlp_order(l):
    o = []
    for f in range(4):
        o += [f"w1_{l}_{f}", f"w2_{l}_{f}"]
    return o


def build_program():
    nc = bass.Bass("TRN2", target_bir_lowering=False)
    dI = lambda n, s: nc.dram_tensor(n, s, F32, kind="ExternalInput").ap()
    dO = lambda n, s: nc.dram_tensor(n, s, F32, kind="ExternalOutput").ap()
    W = dI("W", [len(PIECES), 128, 8, 1024])
    xP = dI("xP", [128, 8, 1024])
    xS = dI("xS", [128, 8, 1056])
    xH = dI("xH", [128, 8, 512])
    cT = dI("cT", [128, 8, 2])
    vecs_d = dI("vecs", [128, NVEC])
    kctx_d = dI("kctx", [128, 4, 256])
    vctx_d = dI("vctx", [128, 2, 4, 64])
    rope_d = dI("rope", [128, 2, 1536])
    masks_d = dI("masks", [128, 4, 512])
    ident_d = dI("ident", [128, 128])
    yP = dO("yP", [128, 8, 1024])
    yS = dO("yS", [128, 8, 1024])
    kPo = dO("kP", [64, 4, 1024])
    vPo = dO("vP", [128, 8, 256])

    S = Sched(nc, 207 * 1024, n_dma_slots=14)
    op, dma = S.op, S.dma

    VEC = S.alloc([NVEC], F32)
    MODS = S.alloc([2, 48, 2], F32)
    COEF = S.alloc([4, 2, 5, 8], F32)
    CTMP = S.alloc([16], F32)
    EPSC = S.alloc([1], F32)
    CIN = S.alloc([8, 2], F32)
    SILUC = S.alloc([8, 2], BF16)
    ONES = S.alloc([128], BF16)
    MASKS = S.alloc([4, 512], BF16)
    IDENT = S.alloc([128], F32)
    ROPE = S.alloc([2, 1536], F32)
    RING = [S.alloc([8, 1024], BF16) for _ in range(4)]
    R = S.alloc([8, 1056], F32)
    H = S.alloc([8, 1056], BF16)
    XBASE = S.sb_ptr
    XEND = S.sbuf_bytes
    print("SBUF persistent bytes", XBASE, "scratch", XEND - XBASE)

    def vcol(name, i=0):
        o = VOFF[name] + i
        return VEC.ap[:, o:o + 1]

    def vrange(name, a, n):
        o = VOFF[name] + a
        return VEC.ap[:, o:o + n]

    bank_rr = {"mm": [0, [0, 1, 2, 3, 6, 7, 4, 5]], "st": [0, [4, 5]], "ln": [0, [4, 5, 6, 7]], "sc": [0, [6, 7, 0, 1, 2, 3, 4, 5]]}

    def bank(pool, width=512):
        st = bank_rr[pool]
        b = st[1][st[0] % len(st[1])]
        st[0] += 1
        return S.psum(b, width)

    ring_state = {"order": [], "next_load": 0, "slot_of": {}, "use": 0, "free": list(range(4))}

    def ring_extend(names):
        ring_state["order"].extend(names)

    def ring_issue():
        rs = ring_state
        i = rs["next_load"]
        name = rs["order"][i]
        si = rs["free"].pop(0)
        slot = RING[si]
        ncols = 256 if name == "wv" else (512 if name == "wkP" else 1024)
        p = PIDX["wk" if name == "wkP" else name]
        for hh in range(2):
            hv = V("sb", slot.lo + hh * 8192, slot.lo + (hh + 1) * 8192, None)
            dma("pool", slot.ap[:, hh * 4:(hh + 1) * 4, 0:ncols], W[p, :, hh * 4:(hh + 1) * 4, 0:ncols], writes=[hv])
        rs["slot_of"][i] = (slot, si)
        rs["next_load"] += 1

    def prefetch():
        rs = ring_state
        while rs["free"] and rs["next_load"] < len(rs["order"]):
            ring_issue()

    def need(name):
        rs = ring_state
        i = rs["use"]
        assert rs["order"][i] == name, (rs["order"][i], name)
        while rs["next_load"] <= i:
            assert rs["free"], ("ring full", name)
            ring_issue()
        rs["use"] += 1
        return rs["slot_of"][i][0]

    def release(slot):
        si = [k for k in range(4) if RING[k] is slot][0]
        assert si not in ring_state["free"]
        ring_state["free"].append(si)
        prefetch()

    dma("sp", VEC.ap, vecs_d, writes=[VEC])
    dma("sp", CIN.ap, cT, writes=[CIN])
    dma("sp", ROPE.ap, rope_d, writes=[ROPE])
    dma("pool", MASKS.ap, masks_d, writes=[MASKS])
    dma("sp", IDENT.ap, ident_d, writes=[IDENT])
    op("dve", lambda h: h.memset(EPSC.ap, EPS), writes=[EPSC])
    op("dve", lambda h: h.memset(ONES.ap, 1.0), writes=[ONES])
    op("act", lambda h: h.activation(SILUC.ap, CIN.ap, AF.Silu), reads=[CIN], writes=[SILUC])
    op("act", lambda h: h.activation(vrange("es", 0, 16), vrange("es", 0, 16), AF.Exp), reads=[VEC], writes=[VEC])

    def ada(l, blocks=range(6)):
        for j in blocks:
            slot = need(f"ada{l}_{j}")
            for m in range(8):
                ps = bank("mm", 2)
                for kc in range(8):
                    op("pe", lambda h, ps=ps, slot=slot, m=m, kc=kc: h.matmul(
                        ps.ap, slot.ap[:, kc, m * 128:(m + 1) * 128], SILUC.ap[:, kc, :], start=(kc == 0), stop=(kc == 7)),
                        reads=[slot, SILUC], writes=[ps], inc=(kc == 7))
                op("dve", lambda h, ps=ps, l=l, j=j, m=m: h.tensor_scalar(
                    MODS.ap[:, l, j * 8 + m, :], ps.ap, vcol(f"ada_b{l}", j * 8 + m), None, ALU.add),
                    reads=[ps, VEC], writes=[MODS])
            release(slot)

    def ada1_bufs():
        return [S.alloc([8, 512], BF16, at=XBASE + 43008 + i * 8192) for i in range(3)]

    def ada1_load(bufs, hp):
        j, half = hp // 2, hp % 2
        buf = bufs[hp % 3]
        dma("pool", buf.ap, W[PIDX[f"ada1_{j}"], :, :, half * 512:(half + 1) * 512], writes=[buf])

    def ada1_compute(bufs, hp):
        j, half = hp // 2, hp % 2
        buf = bufs[hp % 3]
        for m4 in range(4):
            m = half * 4 + m4
            ps = bank("mm", 2)
            for kc in range(8):
                op("pe", lambda h, ps=ps, m4=m4, kc=kc: h.matmul(
                    ps.ap, buf.ap[:, kc, m4 * 128:(m4 + 1) * 128], SILUC.ap[:, kc, :], start=(kc == 0), stop=(kc == 7)),
                    reads=[buf, SILUC], writes=[ps], inc=(kc == 7))
            op("dve", lambda h, ps=ps, j=j, m=m: h.tensor_scalar(
                MODS.ap[:, 1, j * 8 + m, :], ps.ap, vcol("ada_b1", j * 8 + m), None, ALU.add),
                reads=[ps, VEC], writes=[MODS])

    def mod(l, j, cond):
        return MODS.ap[:, l, j * 8:(j + 1) * 8, cond]

    def coef(sub, cond, l, jsh, jsc, jg, lng, lnb, bias, part="all"):
        C = lambda k: COEF.ap[:, sub, cond, k, :]
        t1 = CTMP.ap[:, 0:8]
        rd, wr = [MODS, VEC, COEF, CTMP], [COEF, CTMP]
        if part == "g":
            assert lng is None and bias is None
            op("dve", lambda h: h.tensor_copy(C(4), mod(l, jg, cond)), reads=rd, writes=wr)
            return
        op("dve", lambda h: h.tensor_scalar(t1, mod(l, jsc, cond), 1.0, None, ALU.add), reads=rd, writes=wr)
        if lng is None:
            op("dve", lambda h: h.tensor_copy(C(0), t1), reads=rd, writes=wr)
            op("dve", lambda h: h.tensor_copy(C(1), mod(l, jsh, cond)), reads=rd, writes=wr)
            op("dve", lambda h: h.memset(C(2), ALPHA), reads=rd, writes=wr)
            if bias is None:
                op("dve", lambda h: h.memset(C(3), 0.0), reads=rd, writes=wr)
            else:
                op("dve", lambda h: h.tensor_tensor(C(3), mod(l, jg, cond), vrange(bias, 0, 8), ALU.mult), reads=rd, writes=wr)
            if part == "h":
                return
        else:
            g, b = vrange(lng, 0, 8), vrange(lnb, 0, 8)
            op("dve", lambda h: h.tensor_tensor(C(0), t1, g, ALU.mult), reads=rd, writes=wr)
            op("dve", lambda h: h.tensor_tensor(C(1), t1, b, ALU.mult), reads=rd, writes=wr)
            op("dve", lambda h: h.tensor_tensor(C(1), C(1), mod(l, jsh, cond), ALU.add), reads=rd, writes=wr)
            op("dve", lambda h: h.tensor_scalar(C(2), g, ALPHA, None, ALU.mult), reads=rd, writes=wr)
            if bias is None:
                op("dve", lambda h: h.tensor_scalar(C(3), b, ALPHA, None, ALU.mult), reads=rd, writes=wr)
            else:
                op("dve", lambda h: h.tensor_tensor(C(3), mod(l, jg, cond), vrange(bias, 0, 8), ALU.mult), reads=rd, writes=wr)
                op("dve", lambda h: h.scalar_tensor_tensor(C(3), b, ALPHA, C(3), ALU.mult, ALU.add), reads=rd, writes=wr)
        op("dve", lambda h: h.tensor_copy(C(4), mod(l, jg, cond)), reads=rd, writes=wr)

    def cf(sub, cond, k, c):
        return COEF.ap[:, sub, cond, k, c:c + 1]

    def prep(sub, cond, a0, tw, src=None):
        for c in range(8):
            sv_ = R.v(c, a0, tw) if src is None else src.v(c, 0, tw)
            op("act", lambda h, c=c, sv_=sv_: h.activation(H.ap[:, c, a0:a0 + tw], sv_.ap, AF.Identity,
                                                           bias=cf(sub, cond, 1, c), scale=cf(sub, cond, 0, c)),
               reads=[sv_, COEF], writes=[H.v(c, a0, tw)])
        for c in range(8):
            sv_ = R.v(c, a0, tw) if src is None else src.v(c, 0, tw)
            if c % 4 != 3:
                op("dve", lambda h, c=c, sv_=sv_: h.tensor_scalar(R.ap[:, c, a0:a0 + tw], sv_.ap,
                                                                  cf(sub, cond, 2, c), cf(sub, cond, 3, c), ALU.mult, ALU.add),
                   reads=[sv_, COEF], writes=[R.v(c, a0, tw)])
            else:
                op("act", lambda h, c=c, sv_=sv_: h.activation(R.ap[:, c, a0:a0 + tw], sv_.ap, AF.Identity,
                                                               bias=cf(sub, cond, 3, c), scale=cf(sub, cond, 2, c)),
                   reads=[sv_, COEF], writes=[R.v(c, a0, tw)])

    def linear(slot, col0, nm, act, a0, tw, evac, nk=8, pool="mm", kbase=0):
        for m in range(nm):
            ps = bank(pool, tw)
            for kc in range(nk):
                op("pe", lambda h, ps=ps, m=m, kc=kc: h.matmul(
                    ps.ap, slot.ap[:, kbase + kc, col0 + m * 128: col0 + (m + 1) * 128], act.ap[:, kc, a0:a0 + tw],
                    start=(kc == 0), stop=(kc == nk - 1)),
                    reads=[slot, act.v(kc, a0, tw)], writes=[ps], inc=(kc == nk - 1))
            evac(m, ps)

    def ln_tmps(tw, tmp_at):
        ZB = S.alloc([8, tw], BF16, at=tmp_at)
        ZQ = S.alloc([8, tw], BF16, at=tmp_at + 8 * tw * 2)
        MEAN = S.alloc([tw], F32, at=tmp_at + 16 * tw * 2)
        RSTD = S.alloc([tw], F32, at=tmp_at + 16 * tw * 2 + tw * 4)
        return ZB, ZQ, MEAN, RSTD

    def ln_pre_chunk(src, c, a0, tw, tmp_at, copy_eng="act"):
        ZB, ZQ, _, _ = ln_tmps(tw, tmp_at)
        op("act", lambda h: h.activation(ZQ.ap[:, c, :], src.ap[:, c, a0:a0 + tw], AF.Square),
           reads=[src.v(c, a0, tw)], writes=[ZQ.v(c, 0, tw)])
        if copy_eng == "act":
            op("dve", lambda h: h.tensor_copy(ZB.ap[:, c, :], src.ap[:, c, a0:a0 + tw]),
               reads=[src.v(c, a0, tw)], writes=[ZB.v(c, 0, tw)])
        elif copy_eng == "actcopy":
            op("act", lambda h: h.activation(ZB.ap[:, c, :], src.ap[:, c, a0:a0 + tw], AF.Copy),
               reads=[src.v(c, a0, tw)], writes=[ZB.v(c, 0, tw)])
        else:
            op("pool", lambda h: h.tensor_copy(ZB.ap[:, c, :], src.ap[:, c, a0:a0 + tw]),
               reads=[src.v(c, a0, tw)], writes=[ZB.v(c, 0, tw)])

    def accum(sub, cond, a0, tw, ln_pre=False):
        def ev(m, ps):
            op("dve", lambda h: h.scalar_tensor_tensor(R.ap[:, m, a0:a0 + tw], ps.ap, cf(sub, cond, 4, m), R.ap[:, m, a0:a0 + tw],
                                                       ALU.mult, ALU.add),
               reads=[ps, COEF, R.v(m, a0, tw)], writes=[R.v(m, a0, tw)])
            if ln_pre:
                ln_pre_chunk(R, m, a0, tw, LN_TMP, copy_eng="actcopy")
        return ev

    def layer_norm(src, a0, tw, tmp_at, dst=None, copy_eng="act", pre_done=False):
        dst = src if dst is None else dst
        ZB, ZQ, MEAN, RSTD = ln_tmps(tw, tmp_at)
        if not pre_done:
            for c in range(8):
                ln_pre_chunk(src, c, a0, tw, tmp_at, copy_eng)
        p1, p2 = bank("ln", tw), bank("ln", tw)
        for c in range(8):
            op("pe", lambda h, c=c: h.matmul(p1.ap, ONES.ap, ZB.ap[:, c, :], start=(c == 0), stop=(c == 7)),
               reads=[ONES, ZB.v(c, 0, tw)], writes=[p1], inc=(c == 7))
        for c in range(8):
            op("pe", lambda h, c=c: h.matmul(p2.ap, ONES.ap, ZQ.ap[:, c, :], start=(c == 0), stop=(c == 7)),
               reads=[ONES, ZQ.v(c, 0, tw)], writes=[p2], inc=(c == 7))
        op("dve", lambda h: h.tensor_scalar(MEAN.ap, p1.ap, 1.0 / D, None, ALU.mult), reads=[p1], writes=[MEAN])
        op("dve", lambda h: h.tensor_tensor(RSTD.ap, MEAN.ap, MEAN.ap, ALU.mult), reads=[MEAN], writes=[RSTD])
        op("dve", lambda h: h.scalar_tensor_tensor(RSTD.ap, p2.ap, 1.0 / D, RSTD.ap, ALU.mult, ALU.subtract),
           reads=[p2, RSTD], writes=[RSTD])
        op("act", lambda h: h.activation(RSTD.ap, RSTD.ap, AF.Ln, bias=EPSC.ap[:, 0:1]), reads=[RSTD, EPSC], writes=[RSTD])
        op("act", lambda h: h.activation(RSTD.ap, RSTD.ap, AF.Exp, scale=-0.5), reads=[RSTD], writes=[RSTD])
        for c in range(8):
            op("dve", lambda h, c=c: h.tensor_tensor(dst.ap[:, c, a0:a0 + tw], src.ap[:, c, a0:a0 + tw], MEAN.ap, ALU.subtract),
               reads=[src.v(c, a0, tw), MEAN], writes=[dst.v(c, a0, tw)])
        for c in range(8):
            op("dve", lambda h, c=c: h.tensor_tensor(dst.ap[:, c, a0:a0 + tw], dst.ap[:, c, a0:a0 + tw], RSTD.ap, ALU.mult),
               reads=[dst.v(c, a0, tw), RSTD], writes=[dst.v(c, a0, tw)])

    def mlp(l, sub, cond, tiles, on_done=None, stage_hook=None):
        HID = [S.alloc([8, 512], BF16, at=XBASE + i * 8192) for i in range(2)]
        HT = [S.alloc([512], F32, at=XBASE + 36864 + i * 2048) for i in range(3)]
        ht_rr = [0]
        work = [(f, t) for f in range(4) for t in tiles]
        slots = {}

        def stage1(i):
            f, (a0, tw) = work[i]
            if stage_hook is not None:
                stage_hook(i)
            if f not in slots:
                slots[f] = (need(f"w1_{l}_{f}"), need(f"w2_{l}_{f}"))
            hid = HID[i % 2]

            def ev(m, ps):
                ht = HT[ht_rr[0] % 3]
                ht_rr[0] += 1
                op("act", lambda h: h.activation(ht.ap[:, 0:tw], ps.ap, AF.Relu, bias=vcol(f"b1_{l}", f * 8 + m)),
                   reads=[ps, VEC], writes=[ht])
                op("dve", lambda h: h.scalar_tensor_tensor(hid.ap[:, m, 0:tw], ps.ap, vcol(f"b1_{l}", f * 8 + m), ht.ap[:, 0:tw],
                                                           ALU.add, ALU.mult),
                   reads=[ps, VEC, ht], writes=[hid.v(m, 0, tw)])
            linear(slots[f][0], 0, 8, H, a0, tw, ev)
            if i + 1 == len(work) or work[i + 1][0] != f:
                release(slots[f][0])

        def stage2(i):
            f, (a0, tw) = work[i]
            hid = HID[i % 2]
            ev = accum(sub, cond, a0, tw, ln_pre=(f == 3))
            for m in range(8):
                ps = bank("mm", tw)
                for kc in range(8):
                    op("pe", lambda h, ps=ps, m=m, kc=kc: h.matmul(
                        ps.ap, slots[f][1].ap[:, kc, m * 128:(m + 1) * 128], hid.ap[:, kc, 0:tw], start=(kc == 0), stop=(kc == 7)),
                        reads=[slots[f][1], hid.v(kc, 0, tw)], writes=[ps], inc=(kc == 7))
                ev(m, ps)
            if i + 1 == len(work) or work[i + 1][0] != f:
                release(slots[f][1])
            if f == 3 and on_done is not None:
                on_done(work[i][1])

        for i in range(len(work)):
            stage1(i)
            if i > 0:
                stage2(i - 1)
        stage2(len(work) - 1)

    LN_TMP = XBASE + 16384

    def attn_bufs(nq, nkb, sphase):
        p = XBASE
        b = {}

        def A(name, shape, dt):
            nonlocal p
            b[name] = S.alloc(shape, dt, at=p)
            p = (p + int(np.prod(shape)) * DTSZ[dt] + 63) // 64 * 64
        A("Q", [8, nq], BF16)
        A("PT", [8, 512], BF16)
        A("REC", [1024], F32)
        A("KT", [4, nkb * 128], BF16)
        A("VA", [nkb, 4, 128], BF16)
        if sphase:
            A("RT", [4, 352], F32)
            A("KC", [4, 256], BF16)
            A("VC", [2, 4, 128], BF16)
        b["end"] = p
        return b

    pt_rr = [0]
    rt_rr = [0]
    rec_rr = [0]
    attn_pending = []
    GORD = [0, 2, 1, 3]

    scp_rr = [0]
    SC_PAIRS = [(0, 1), (2, 3), (6, 7)]

    def attn_core(B, j, qcol, nqw, ocol, ktiles):
        po = bank("st", 4 * nqw)
        nk = len(ktiles)
        ptvs = []

        def scores(ti):
            kfn, va_ap, va_v, mask = ktiles[ti]
            pi = pt_rr[0] % 8
            pt_rr[0] += 1
            ptv = B["PT"].v(pi, 0, 4 * nqw)
            ptvs.append(ptv)
            b0, b1 = SC_PAIRS[scp_rr[0] % 3]
            scp_rr[0] += 1
            pss = [S.psum(b0, 2 * nqw), S.psum(b1, 2 * nqw)]
            for half in range(2):
                ps = pss[half]
                k_ap, k_v = kfn(half)
                for gi in range(2):
                    hd = 4 * j + GORD[half * 2 + gi]
                    c = hd // 2
                    assert hd % 2 == half
                    op("pe", lambda h, ps=ps, gi=gi, c=c, half=half, k_ap=k_ap: h.matmul(
                        ps.ap[:, gi * nqw:(gi + 1) * nqw], k_ap, B["Q"].ap[half * 64:(half + 1) * 64, c, qcol:qcol + nqw],
                        start=True, stop=True),
                        reads=[k_v, B["Q"].v(c, qcol, nqw)], writes=[ps], inc=(gi == 1))
            src = S.ps[:, b0 * 512:(b0 + 2) * 512].rearrange("p (b n) -> p b n", b=2)[:, :, 0:2 * nqw]
            dst = ptv.ap.rearrange("p (b n) -> p b n", b=2)
            op("act", lambda h, src=src, dst=dst: h.activation(dst, src, AF.Exp, scale=SCALE), reads=pss, writes=[ptv])
            if mask is not None:
                mv, mq0 = mask
                m_ap = MASKS.ap[:, mv, :].rearrange("p (g q) -> p g q", g=4)[:, :, mq0:mq0 + nqw]
                p_ap = ptv.ap.rearrange("p (g q) -> p g q", g=4)
                op("dve", lambda h, p_ap=p_ap, m_ap=m_ap: h.tensor_tensor(p_ap, p_ap, m_ap, ALU.mult),
                   reads=[ptv, MASKS], writes=[ptv])

        def pv(ti):
            kfn, va_ap, va_v, mask = ktiles[ti]
            ptv = ptvs[ti]
            op("pe", lambda h, ptv=ptv, va_ap=va_ap, ti=ti: h.matmul(po.ap, va_ap, ptv.ap, start=(ti == 0), stop=(ti == nk - 1)),
               reads=[va_v, ptv], writes=[po], inc=(ti == nk - 1))

        for ti in range(min(3, nk)):
            scores(ti)
        for ti in range(nk):
            pv(ti)
            if ti + 3 < nk:
                scores(ti + 3)
        attn_pending.append(lambda: attn_norm(B, j, nqw, ocol, po))
        if len(attn_pending) > 1:
            attn_pending.pop(0)()

    def attn_flush():
        while attn_pending:
            attn_pending.pop(0)()

    def attn_norm(B, j, nqw, ocol, po):
        rec = B["REC"].cols((rec_rr[0] % 2) * 512, 4 * nqw)
        rec_rr[0] += 1
        eo = VOFF["es"] + 4 * j
        es_b = VEC.ap[64:128, eo:eo + 4].unsqueeze(2).to_broadcast([64, 4, nqw])
        r3 = rec.ap[64:128, :].rearrange("p (g q) -> p g q", g=4)
        op("dve", lambda h: h.tensor_tensor(r3, po.ap[64:128, :].rearrange("p (g q) -> p g q", g=4), es_b, ALU.add),
           reads=[po, VEC], writes=[rec])
        op("act", lambda h: h.activation(rec.ap[64:128, :], rec.ap[64:128, :], AF.Ln), reads=[rec], writes=[rec])
        op("act", lambda h: h.activation(rec.ap[64:128, :], rec.ap[64:128, :], AF.Exp, scale=-1.0), reads=[rec], writes=[rec])
        for half in range(2):
            sl = slice(half * 2 * nqw, (half + 1) * 2 * nqw)
            op("dve", lambda h, half=half, sl=sl: h.tensor_tensor(
                H.ap[half * 64:(half + 1) * 64, 2 * j:2 * j + 2, ocol:ocol + nqw],
                po.ap[0:64, sl].rearrange("p (g q) -> p g q", g=2),
                rec.ap[64:128, sl].rearrange("p (g q) -> p g q", g=2), ALU.mult),
                reads=[po, rec], writes=[H.v(2 * j, ocol, nqw), H.v(2 * j + 1, ocol, nqw)])

    def v_block(B, slot, src, scol, blk, kout=None):
        ps = bank("mm", 256)
        for kc in range(8):
            op("pe", lambda h, kc=kc: h.matmul(ps.ap, src.ap[:, kc, scol:scol + 128], slot.ap[:, kc, 0:256], start=(kc == 0), stop=(kc == 7)),
               reads=[slot, src.v(kc, scol, 128)], writes=[ps], inc=(kc == 7))
        va = B["VA"]
        lo = va.lo + blk * 4 * 128 * 2
        vav = V("sb", lo, lo + 4 * 128 * 2, va.ap[:, blk, :, 0:64])
        op("act", lambda h: h.activation(vav.ap, ps.ap.rearrange("p (j d) -> p j d", j=4), AF.Copy), reads=[ps], writes=[vav])
        if kout is not None:
            op("dve", lambda h: h.tensor_copy(kout.ap[:, blk, :], ps.ap), reads=[ps], writes=[kout.v(blk, 0, 256)])

    def va_view(B, blk, j):
        va = B["VA"]
        lo = va.lo + (blk * 4 + j) * 128 * 2
        return va.ap[:, blk, j, :], V("sb", lo, lo + 256, None)

    ring_extend(["ada0_0", "ada0_1", "wq", "wkP", "wv"] + [f"ada0_{j}" for j in range(2, 6)] + ["wo"] + _mlp_order(0) + ["pw1a", "pw1g", "pw2"] + _mlp_order(1))
    ring_extend(["wk", "wv", "wq", "wqs", "wo"] + _mlp_order(0) + ["pw1a", "pw1g", "pw2"] + _mlp_order(1))
    prefetch()
    ada(0, [0, 1])

    def conv_sublayer(cond, T_in, in_tiles, segs, out_tiles, ucols):
        ACC = S.alloc([8, 1024], F32, at=XBASE)
        lnp = XBASE + 32768
        p = XBASE + 43008
        UP = S.alloc([8, ucols], BF16, at=p)
        p = (p + 8 * ucols * 2 + 63) // 64 * 64
        SG = [S.alloc([512], F32, at=p + i * 2048) for i in range(2)]
        p += 4096
        DG = S.alloc([16, 128], BF16, at=p)
        p += 4096
        assert p <= XEND, (p, XEND)
        sa, sg_ = need("pw1a"), need("pw1g")
        op("dve", lambda h: h.memset(UP.ap, 0.0), writes=[UP])
        def upcol(col):
            for (c0, ln, u0) in segs:
                if c0 <= col < c0 + ln:
                    return u0 + (col - c0)
            raise AssertionError(col)
        sgi = [0]
        for (a0, tw) in in_tiles:
            parts = []
            x0 = a0
            while x0 < a0 + tw:
                for (c0, ln, u0) in segs:
                    if c0 <= x0 < c0 + ln:
                        e = min(a0 + tw, c0 + ln)
                        parts.append((x0, e - x0, u0 + x0 - c0))
                        x0 = e
                        break
                else:
                    raise AssertionError
            for m in range(8):
                psg = bank("mm", tw)
                for kc in range(8):
                    op("pe", lambda h, psg=psg, m=m, kc=kc: h.matmul(psg.ap, sg_.ap[:, kc, m * 128:(m + 1) * 128], H.ap[:, kc, a0:a0 + tw],
                                                                     start=(kc == 0), stop=(kc == 7)),
                       reads=[sg_, H.v(kc, a0, tw)], writes=[psg], inc=(kc == 7))
                sg = SG[sgi[0] % 2]
                sgi[0] += 1
                op("act", lambda h, psg=psg, sg=sg, m=m: h.activation(sg.ap[:, 0:tw], psg.ap, AF.Sigmoid, bias=vcol("pw1_b", 8 + m)),
                   reads=[psg, VEC], writes=[sg])
                psa = bank("mm", tw)
                for kc in range(8):
                    op("pe", lambda h, psa=psa, m=m, kc=kc: h.matmul(psa.ap, sa.ap[:, kc, m * 128:(m + 1) * 128], H.ap[:, kc, a0:a0 + tw],
                                                                     start=(kc == 0), stop=(kc == 7)),
                       reads=[sa, H.v(kc, a0, tw)], writes=[psa], inc=(kc == 7))
                for (x0, ln, u0) in parts:
                    op("dve", lambda h, psa=psa, sg=sg, m=m, x0=x0, ln=ln, u0=u0: h.scalar_tensor_tensor(
                        UP.ap[:, m, u0:u0 + ln], psa.ap[:, x0 - a0:x0 - a0 + ln], vcol("pw1_b", m), sg.ap[:, x0 - a0:x0 - a0 + ln],
                        ALU.add, ALU.mult),
                        reads=[psa, sg, VEC], writes=[UP.v(m, u0, ln)])
        release(sa); release(sg_)
        return UP, ACC, lnp, DG

    dg_rr = [0]

    def conv_taps(UP, ACC, lnp, DG, out_specs):
        for c in range(8):
            pss = [bank("sc", tw) for (_, tw, _) in out_specs]
            for w in range(31):
                di = dg_rr[0] % 16
                dg_rr[0] += 1
                dv = DG.v(di, 0, 128)
                if w % 2 == 0:
                    op("act", lambda h, dv=dv, w=w, c=c: h.activation(dv.ap, IDENT.ap, AF.Identity, scale=vcol("dw_w", w * 8 + c)),
                       reads=[IDENT, VEC], writes=[dv])
                else:
                    op("dve", lambda h, dv=dv, w=w, c=c: h.tensor_scalar(dv.ap, IDENT.ap, vcol("dw_w", w * 8 + c), None, ALU.mult),
                       reads=[IDENT, VEC], writes=[dv])
                for oi, (ps, (uc0, tw, d0)) in enumerate(zip(pss, out_specs)):
                    op("pe", lambda h, ps=ps, dv=dv, c=c, w=w, uc0=uc0, tw=tw: h.matmul(
                        ps.ap, dv.ap, UP.ap[:, c, uc0 - 15 + w:uc0 - 15 + w + tw], start=(w == 0), stop=(w == 30)),
                        reads=[dv, UP.v(c, uc0 - 15 + w, tw)], writes=[ps], inc=(w == 30 or oi == len(pss) - 1))
            for ps, (uc0, tw, d0) in zip(pss, out_specs):
                op("dve", lambda h, ps=ps, c=c, d0=d0, tw=tw: h.tensor_scalar(ACC.ap[:, c, d0:d0 + tw], ps.ap, vcol("dw_b", c), None, ALU.add),
                   reads=[ps, VEC], writes=[ACC.v(c, d0, tw)])

    def conv_post(ACC, lnp, own0):
        for i in range(4):
            d0 = i * 256
            layer_norm(ACC, d0, 256, lnp, copy_eng="pool")
            for c in range(8):
                op("act", lambda h, c=c, d0=d0: h.activation(H.ap[:, c, own0 + d0:own0 + d0 + 256], ACC.ap[:, c, d0:d0 + 256], AF.Silu,
                                                             bias=vcol("cn_b", c), scale=vcol("cn_g", c)),
                   reads=[ACC.v(c, d0, 256), VEC], writes=[H.v(c, own0 + d0, 256)])

    XSTAGE = [S.alloc([8, 352], F32, at=XBASE + 43008 + i * 11264) for i in range(2)]

    def run_phase(ph):
        cond = ph
        if ph == 0:
            T_, tiles, own0 = 1024, [(0, 512), (512, 512)], 0
            xin = xP
        else:
            T_, tiles, own0 = 1056, [(0, 352), (352, 352), (704, 352)], 16
            xin = xS
        own_tiles = [(own0, 512), (own0 + 512, 512)]
        xsrc = {}
        for ti, (a0, tw) in enumerate(tiles):
            if ph == 1 and ti < 2:
                xsrc[a0] = XSTAGE[ti]
            else:
                dma("sp", R.ap[:, :, a0:a0 + tw], xin[:, :, a0:a0 + tw], writes=R.vs(a0, tw))
        coef(0, cond, 0, 0, 1, 2, None, None, None, part=("h" if ph == 0 else "all"))
        B = attn_bufs(T_, 8 if ph == 0 else 12, ph == 1)
        if ph == 1:
            for (a0, tw) in tiles:
                prep(0, cond, a0, tw, src=xsrc.get(a0))
        op("pool", lambda h: h.memset(B["VA"].ap[:, :, :, 64:128], 1.0), writes=[B["VA"]])
        if ph == 1:
            op("pool", lambda h: h.memset(B["VC"].ap[:, :, :, 64:128], 1.0), writes=[B["VC"]])
        if ph == 0:
            KOUT = S.alloc([4, 1024], F32, at=B["end"])
            VOUT = S.alloc([8, 256], F32, at=B["end"] + 16384)
            assert B["end"] + 16384 + 8192 <= XEND
            for (a0, tw) in tiles:
                prep(0, cond, a0, tw)
            sq, sk, sv = need("wq"), need("wkP"), need("wv")
            for (a0, tw) in tiles:
                def evq(m, ps, a0=a0, tw=tw):
                    op("act", lambda h: h.activation(B["Q"].ap[:, m, a0:a0 + tw], ps.ap, AF.Copy), reads=[ps], writes=[B["Q"].v(m, a0, tw)])
                linear(sq, 0, 8, H, a0, tw, evq)

                def evk(m, ps, a0=a0, tw=tw):
                    op("act", lambda h: h.activation(B["KT"].ap[:, m, a0:a0 + tw], ps.ap, AF.Copy), reads=[ps], writes=[B["KT"].v(m, a0, tw)])
                    op("dve", lambda h: h.tensor_copy(KOUT.ap[0:64, m, a0:a0 + tw], ps.ap[0:64, :]), reads=[ps], writes=[KOUT.v(m, a0, tw)])
                linear(sk, 0, 4, H, a0, tw, evk)
                for blk in range(a0 // 128, (a0 + tw) // 128):
                    v_block(B, sv, H, blk * 128, blk, kout=VOUT)
            release(sq); release(sk); release(sv)
            dma("sp", kPo, KOUT.ap[0:64, :, :], reads=[KOUT])
            dma("sp", vPo, VOUT.ap, reads=[VOUT])
            for s in range(4):
                ada(0, [2 + s])
                for j in range(4):
                    kts = []
                    for kt in range(2):
                        col = s * 256 + kt * 128
                        kfn = (lambda half, col=col, j=j: (B["KT"].ap[half * 64:(half + 1) * 64, j, col:col + 128], B["KT"].v(j, col, 128)))
                        va_ap, va_v = va_view(B, s * 2 + kt, j)
                        kts.append((kfn, va_ap, va_v, None))
                    for qh in range(2):
                        attn_core(B, j, s * 256 + qh * 128, 128, s * 256 + qh * 128, kts)
            coef(0, cond, 0, 0, 1, 2, None, None, None, part="g")
        else:
            XHT = S.alloc([8, 512], F32, at=XBASE)
            HH = S.alloc([8, 512], BF16, at=XBASE + 16384)
            dma("sp", XHT.ap, xH, writes=[XHT])
            for c in range(8):
                op("act", lambda h, c=c: h.activation(HH.ap[:, c, :], XHT.ap[:, c, :], AF.Identity, bias=cf(0, cond, 1, c), scale=cf(0, cond, 0, c)),
                   reads=[XHT.v(c, 0, 512), COEF], writes=[HH.v(c, 0, 512)])
            sk, sv = need("wk"), need("wv")
            dma("pool", B["KC"].ap, kctx_d, writes=[B["KC"]])
            dma("pool", B["VC"].ap[:, :, :, 0:64], vctx_d, writes=[B["VC"]])
            RT = B["RT"]

            def rope_evac(dst, m, ps_a, ps_b, dcol, tw, rcol):
                rb = (rt_rr[0] % 2) * 2
                rt_rr[0] += 1
                t1, t2 = RT.v(rb, 0, tw), RT.v(rb + 1, 0, tw)
                op("dve", lambda h: h.tensor_tensor(t1.ap, ps_a.ap, ROPE.ap[:, 0, rcol:rcol + tw], ALU.mult), reads=[ps_a, ROPE], writes=[t1])
                op("dve", lambda h: h.tensor_tensor(t2.ap, ps_b.ap, ROPE.ap[:, 1, rcol:rcol + tw], ALU.mult), reads=[ps_b, ROPE], writes=[t2])
                op("pool", lambda h: h.tensor_tensor(dst.ap[:, m, dcol:dcol + tw], t1.ap, t2.ap, ALU.add), reads=[t1, t2], writes=[dst.v(m, dcol, tw)])

            def k_proj(src, scol, tw, ecol):
                for m in range(4):
                    pa, pb = bank("mm", tw), bank("mm", tw)
                    for (pp, cbase) in ((pa, 0), (pb, 512)):
                        for kc in range(8):
                            op("pe", lambda h, pp=pp, cbase=cbase, m=m, kc=kc: h.matmul(
                                pp.ap, sk.ap[:, kc, cbase + m * 128:cbase + (m + 1) * 128], src.ap[:, kc, scol:scol + tw],
                                start=(kc == 0), stop=(kc == 7)),
                                reads=[sk, src.v(kc, scol, tw)], writes=[pp], inc=(kc == 7))
                    rope_evac(B["KT"], m, pa, pb, ecol, tw, ecol)

            k_proj(HH, 0, 256, 0)
            k_proj(HH, 256, 256, 1280)
            for i, blk in enumerate((0, 1, 10, 11)):
                v_block(B, sv, HH, i * 128, blk)
            for (a0, tw) in tiles:
                k_proj(H, a0, tw, 240 + a0)
            for blk in range(2, 10):
                v_block(B, sv, H, 16 + (blk - 2) * 128, blk)
            sq, sqs = need("wq"), need("wqs")
            for (a0, tw) in tiles:
                for m in range(8):
                    pa, pb = bank("mm", tw), bank("mm", tw)
                    for (pp, sl) in ((pa, sq), (pb, sqs)):
                        for kc in range(8):
                            op("pe", lambda h, pp=pp, sl=sl, m=m, kc=kc: h.matmul(
                                pp.ap, sl.ap[:, kc, m * 128:(m + 1) * 128], H.ap[:, kc, a0:a0 + tw], start=(kc == 0), stop=(kc == 7)),
                                reads=[sl, H.v(kc, a0, tw)], writes=[pp], inc=(kc == 7))
                    rope_evac(B["Q"], m, pa, pb, a0, tw, 240 + a0)
            release(sk); release(sv); release(sq); release(sqs)
            def ctx_tiles(j):
                out = []
                for kt in range(2):
                    kfn = (lambda half, kt=kt, j=j: (B["KC"].ap[half * 64:(half + 1) * 64, j, kt * 128:(kt + 1) * 128], B["KC"].v(j, kt * 128, 128)))
                    vc = B["VC"]
                    lo = vc.lo + (kt * 4 + j) * 256
                    out.append((kfn, vc.ap[:, kt, j, :], V("sb", lo, lo + 256, None), None))
                return out

            def loc_tile(e, j, mask):
                kfn = (lambda half, e=e, j=j: (B["KT"].ap[half * 64:(half + 1) * 64, j, e * 128:(e + 1) * 128], B["KT"].v(j, e * 128, 128)))
                va_ap, va_v = va_view(B, e, j)
                return (kfn, va_ap, va_v, mask)

            qblocks = [(1, 0, 16, 112)] + [(n + 2, 16 + n * 128, 128, 0) for n in range(8)] + [(10, 1040, 16, 0)]
            for (e, qcol, nqw, mq0) in qblocks:
                for j in range(4):
                    mprev = 2 if e == 2 else 0
                    mnext = 3 if e == 9 else 1
                    kts = [loc_tile(e - 1, j, (mprev, mq0)), loc_tile(e, j, None), loc_tile(e + 1, j, (mnext, mq0))] + ctx_tiles(j)
                    attn_core(B, j, qcol, nqw, qcol, kts)
        attn_flush()
        def fin(sub_next, t):
            layer_norm(R, t[0], t[1], LN_TMP, pre_done=True)
            prep(sub_next, cond, t[0], t[1])

        coef(1, cond, 0, 3, 4, 5, "ln1_g0", "ln1_b0", "b2_0")
        so = need("wo")
        for i, (a0, tw) in enumerate(tiles):
            linear(so, 0, 8, H, a0, tw, accum(0, cond, a0, tw, ln_pre=True))
            if i == len(tiles) - 1:
                release(so)
            fin(1, tiles[i])
        def coefs_l1():
            coef(2, cond, 1, 0, 1, 2, "ln2_g0", "ln2_b0", "pw2_b")
            coef(3, cond, 1, 3, 4, 5, "ln1_g1", "ln1_b1", "b2_1")

        if ph == 0:
            abufs = ada1_bufs()

            def hook(i):
                if 1 <= i <= 6:
                    ada1_compute(abufs, 2 * (i - 1))
                    ada1_compute(abufs, 2 * (i - 1) + 1)
                if i <= 5:
                    ada1_load(abufs, 2 * i)
                    ada1_load(abufs, 2 * i + 1)
                if i == 7:
                    coefs_l1()
            assert len(tiles) * 4 == 8
            mlp(0, 1, cond, tiles, on_done=lambda t: fin(2, t), stage_hook=hook)
        else:
            coefs_l1()
            mlp(0, 1, cond, tiles, on_done=lambda t: fin(2, t))
        if ph == 0:
            segs = [(s * 256, 256, s * 286 + 15) for s in range(4)]
            ucols = 4 * 286
        else:
            segs = [(0, 1056, 0)]
            ucols = 1056
        UP, ACC, lnp, DG = conv_sublayer(cond, T_, tiles, segs, None, ucols)
        if ph == 1:
            for c in range(8):
                op("dve", lambda h, c=c: h.tensor_scalar(UP.ap[:, c, 0:16], UP.ap[:, c, 0:16], vcol("valid", 0), None, ALU.mult),
                   reads=[UP.v(c, 0, 16), VEC], writes=[UP.v(c, 0, 16)])
                op("dve", lambda h, c=c: h.tensor_scalar(UP.ap[:, c, 1040:1056], UP.ap[:, c, 1040:1056], vcol("valid", 1), None, ALU.mult),
                   reads=[UP.v(c, 1040, 16), VEC], writes=[UP.v(c, 1040, 16)])
            out_specs = [(16 + i * 512, 512, i * 512) for i in range(2)]
        else:
            out_specs = [(s * 286 + 15, 256, s * 256) for s in range(4)]
        conv_taps(UP, ACC, lnp, DG, out_specs)
        conv_post(ACC, lnp, own0)
        if ph == 0:
            for ti in range(2):
                dma("sp", XSTAGE[ti].ap, xS[:, :, ti * 352:(ti + 1) * 352], writes=[XSTAGE[ti]])
        sp2 = need("pw2")
        for i, (a0, tw) in enumerate(own_tiles):
            linear(sp2, 0, 8, H, a0, tw, accum(2, cond, a0, tw, ln_pre=True))
            if i == len(own_tiles) - 1:
                release(sp2)
            fin(3, own_tiles[i])
        yout = yP if ph == 0 else yS

        def final(t):
            a0, tw = t
            layer_norm(R, a0, tw, LN_TMP, pre_done=True)
            for c in range(8):
                op("act", lambda h, c=c: h.activation(R.ap[:, c, a0:a0 + tw], R.ap[:, c, a0:a0 + tw], AF.Identity,
                                                      bias=vcol("ln2_b1", c), scale=vcol("ln2_g1", c)),
                   reads=[R.v(c, a0, tw), VEC], writes=[R.v(c, a0, tw)])
            dma("sp", yout[:, :, a0 - own0:a0 - own0 + tw], R.ap[:, :, a0:a0 + tw], reads=R.vs(a0, tw))

        mlp(1, 3, cond, own_tiles, on_done=final)

    run_phase(0)
    run_phase(1)
    S.finish("sp")
    S.emit()
    S.close()
    return nc


def _fm(v):
    v = np.asarray(v, np.float32).reshape(-1, 128)
    return np.ascontiguousarray(v.T)


def _wpiece(w):
    out = np.zeros((128, 8, 1024), np.float32)
    out[:, :, :w.shape[1]] = w.reshape(8, 128, w.shape[1]).transpose(1, 0, 2)
    return out


_NC_CACHE = {}


def kernel(x_prompt, x_sample, cache_k, cache_v, c, c_ctx, ada_w, ada_b, attn_w_qkv, attn_w_o, attn_sink,
           conv_pw1_w, conv_pw1_b, conv_dw_w, conv_dw_b, conv_norm_g, conv_norm_b, conv_pw2_w, conv_pw2_b,
           ln1_g, ln1_b, mlp_w1, mlp_b1, mlp_w2, mlp_b2, ln2_g, ln2_b):
    f = lambda a: np.asarray(a, np.float32)
    x_prompt, x_sample, cache_k, cache_v, c, c_ctx = map(f, (x_prompt, x_sample, cache_k, cache_v, c, c_ctx))
    ada_w, ada_b, wqkv, wo = f(ada_w), f(ada_b), f(attn_w_qkv)[0], f(attn_w_o)[0]
    Wall = np.zeros((len(PIECES), 128, 8, 1024), np.float32)
    for l in range(2):
        for j in range(6):
            Wall[PIDX[f"ada{l}_{j}"]] = _wpiece(ada_w[l][:, j * 1024:(j + 1) * 1024])
        for fb in range(4):
            Wall[PIDX[f"w1_{l}_{fb}"]] = _wpiece(f(mlp_w1)[l][:, fb * 1024:(fb + 1) * 1024])
            Wall[PIDX[f"w2_{l}_{fb}"]] = _wpiece(f(mlp_w2)[l][fb * 1024:(fb + 1) * 1024, :])
    partner = np.concatenate([np.arange(16, 32), np.arange(0, 16), np.arange(48, 64), np.arange(32, 48)])
    wq = wqkv[:, 0:1024]
    wk = wqkv[:, 1024:1280]
    wv = wqkv[:, 1280:1536]
    qperm = (np.arange(16)[:, None] * 64 + partner[None, :]).reshape(-1)
    Wall[PIDX["wq"]] = _wpiece(wq)
    Wall[PIDX["wqs"]] = _wpiece(wq[:, qperm])
    kd = np.concatenate([np.concatenate([wk[:, j * 64:(j + 1) * 64]] * 2, 1) for j in range(4)], 1)
    ks = np.concatenate([np.concatenate([wk[:, j * 64 + partner]] * 2, 1) for j in range(4)], 1)
    Wall[PIDX["wk"]] = _wpiece(np.concatenate([kd, ks], 1))
    Wall[PIDX["wv"]] = _wpiece(wv)
    Wall[PIDX["wo"]] = _wpiece(wo)
    pw1 = f(conv_pw1_w)[0]
    Wall[PIDX["pw1a"]] = _wpiece(pw1[:, 0:1024])
    Wall[PIDX["pw1g"]] = _wpiece(pw1[:, 1024:2048])
    Wall[PIDX["pw2"]] = _wpiece(f(conv_pw2_w)[0])
    vec = np.zeros((128, NVEC), np.float32)

    def put(name, arr):
        arr = np.asarray(arr, np.float32)
        vec[:, VOFF[name]:VOFF[name] + arr.shape[1]] = arr
    for l in range(2):
        put(f"ada_b{l}", _fm(ada_b[l]))
        put(f"ln1_g{l}", _fm(f(ln1_g)[l])); put(f"ln1_b{l}", _fm(f(ln1_b)[l]))
        put(f"ln2_g{l}", _fm(f(ln2_g)[l])); put(f"ln2_b{l}", _fm(f(ln2_b)[l]))
        put(f"b1_{l}", _fm(f(mlp_b1)[l])); put(f"b2_{l}", _fm(f(mlp_b2)[l]))
    put("pw1_b", _fm(f(conv_pw1_b)[0]))
    put("dw_w", _fm(f(conv_dw_w)[0].reshape(-1)))
    put("dw_b", _fm(f(conv_dw_b)[0])); put("cn_g", _fm(f(conv_norm_g)[0])); put("cn_b", _fm(f(conv_norm_b)[0]))
    put("pw2_b", _fm(f(conv_pw2_b)[0]))
    gperm = np.array([4 * j + g for j in range(4) for g in (0, 2, 1, 3)])
    put("es", np.broadcast_to(f(attn_sink)[0][gperm][None, :], (128, 16)))
    half = 32
    inv = (10000.0 ** (-np.arange(0, half, 2, dtype=np.float32) / half)).astype(np.float32)
    ki = np.arange(128)[:, None]
    qi = np.arange(128)[None, :]
    tri_prev = (ki >= qi).astype(np.float32)
    tri_next = (ki <= qi).astype(np.float32)
    in_maps = []
    for core in range(8):
        b, s0 = core // 4, (core % 4) * 1024
        xp = x_prompt[4 * core:4 * core + 4].reshape(1024, 8, 128).transpose(2, 1, 0)

        def tok(lo, hi):
            out = np.zeros((hi - lo, 1024), np.float32)
            a, e = max(lo, 0), min(hi, 4096)
            out[a - lo:e - lo] = x_sample[b, a:e]
            return out
        xs = tok(s0 - 16, s0 + 1040).reshape(1056, 8, 128).transpose(2, 1, 0)
        xh = np.concatenate([tok(s0 - 256, s0), tok(s0 + 1024, s0 + 1280)], 0).reshape(512, 8, 128).transpose(2, 1, 0)
        ct = np.stack([c_ctx, c[b]], -1).reshape(8, 128, 2).transpose(1, 0, 2)
        v = vec.copy()
        vl, vr = float(s0 > 0), float(s0 + 1024 < 4096)
        v[:, VOFF["valid"]] = vl
        v[:, VOFF["valid"] + 1] = vr
        kc_ = cache_k[b, 0].transpose(2, 1, 0)
        kctx = np.concatenate([kc_, kc_], 0)
        vctx = cache_v[b, 0].reshape(2, 128, 4, 64).transpose(1, 0, 2, 3)
        pos = np.arange(s0 - 256, s0 + 1280)
        row = (pos // 64).astype(np.float32)
        col = (pos % 64).astype(np.float32)
        ang = np.concatenate([row[None, :] * inv[:, None]] * 2 + [col[None, :] * inv[:, None]] * 2, 0)
        cos = np.cos(ang).astype(np.float32)
        sin = np.sin(ang).astype(np.float32)
        sgn = np.concatenate([-np.ones(16), np.ones(16), -np.ones(16), np.ones(16)]).astype(np.float32)[:, None]
        rope = np.stack([np.concatenate([cos, cos], 0), np.concatenate([sin * sgn, sin * sgn], 0)], 1)
        mk = np.stack([np.tile(tri_prev, (1, 4)), np.tile(tri_next, (1, 4)),
                       np.tile(tri_prev, (1, 4)) * vl, np.tile(tri_next, (1, 4)) * vr], 1)
        in_maps.append({"W": Wall, "xP": np.ascontiguousarray(xp), "xS": np.ascontiguousarray(xs), "xH": np.ascontiguousarray(xh),
                        "cT": np.ascontiguousarray(ct), "vecs": v, "kctx": np.ascontiguousarray(kctx),
                        "vctx": np.ascontiguousarray(vctx), "rope": np.ascontiguousarray(rope.astype(np.float32)),
                        "masks": np.ascontiguousarray(mk.astype(np.float32)), "ident": np.eye(128, dtype=np.float32)})
    if "nc" not in _NC_CACHE:
        _NC_CACHE["nc"] = build_program()
    nc = _NC_CACHE["nc"]
    res = run_bass_kernel_spmd(nc, in_maps, core_ids=list(range(8)))
    y_prompt = np.zeros((32, 256, 1024), np.float32)
    y_sample = np.zeros((2, 4096, 1024), np.float32)
    new_k = np.zeros((32, 1, 256, 4, 64), np.float32)
    new_v = np.zeros((32, 1, 256, 4, 64), np.float32)
    for core in range(8):
        r = res.results[core]
        b, s0 = core // 4, (core % 4) * 1024
        y_prompt[4 * core:4 * core + 4] = r["yP"].transpose(2, 1, 0).reshape(4, 256, 1024)
        y_sample[b, s0:s0 + 1024] = r["yS"].transpose(2, 1, 0).reshape(1024, 1024)
        new_k[4 * core:4 * core + 4, 0] = r["kP"].transpose(2, 1, 0).reshape(4, 256, 4, 64)
        new_v[4 * core:4 * core + 4, 0] = r["vP"].transpose(1, 0, 2).reshape(4, 256, 4, 64)
    return (y_prompt, y_sample, new_k, new_v)
```

```python
import numpy as np
import concourse.bass as bass
import concourse.mybir as mybir
from concourse.bass_utils import run_bass_kernel_spmd

F32 = mybir.dt.float32
BF16 = mybir.dt.bfloat16
AF = mybir.ActivationFunctionType
ALU = mybir.AluOpType
DTSZ = {F32: 4, BF16: 2}
CELL = 32

D = 1024
NCH = 8
ALPHA = 4.0 ** 0.25
EPS = 1e-5
SCALE = 0.125


class V:
    def __init__(self, space, lo, hi, ap):
        self.space, self.lo, self.hi, self.ap = space, lo, hi, ap


class T(V):
    def __init__(self, space, lo, hi, ap, shape, esz):
        super().__init__(space, lo, hi, ap)
        self.shape, self.esz = shape, esz

    def v(self, c, a0, w):
        n = self.shape[-1]
        lo = self.lo + (c * n + a0) * self.esz
        return V(self.space, lo, lo + w * self.esz, self.ap[:, c, a0:a0 + w])

    def vs(self, a0, w):
        return [self.v(c, a0, w) for c in range(self.shape[0])]

    def cols(self, a0, w):
        lo = self.lo + a0 * self.esz
        return V(self.space, lo, lo + w * self.esz, self.ap[:, a0:a0 + w])


class _Rec:
    def __init__(self):
        self.call = None

    def __getattr__(self, name):
        def f(*a, **k):
            self.call = (name, a, k)
        return f


def _eager(fn):
    r = _Rec()
    fn(r)
    name, a, k = r.call
    return lambda h: getattr(h, name)(*a, **k)


class Stream:
    def __init__(self, name, sem, idx):
        self.name, self.sem, self.idx, self.val = name, sem, idx, 0


class Engine:
    def __init__(self, name, stream, self_sync):
        self.name, self.stream, self.self_sync = name, stream, self_sync
        self.ops, self.seen, self.pending_noinc = [], {}, False


class Sched:
    def __init__(self, nc, sbuf_bytes, n_dma_slots=10):
        self.nc, self._cm, self.streams, self.eng = nc, [], [], {}
        for name, ss in (("pe", False), ("act", True), ("dve", True), ("pool", True), ("sp", False)):
            self.eng[name] = Engine(name, self._new_stream(name), ss)
        self.dma_slots = {"hw": [self._new_stream(f"dma{i}") for i in range(n_dma_slots)],
                          "sw": [self._new_stream(f"swdma{i}") for i in range(n_dma_slots)]}
        self.dma_rr = {"hw": 0, "sw": 0}
        ns = len(self.streams)
        self.sbuf_bytes = sbuf_bytes
        self.ncell = {"sb": sbuf_bytes // CELL + 1, "ps": 16384 // CELL}
        self.wv = {sp: np.zeros((ns, n), np.int64) for sp, n in self.ncell.items()}
        self.rv = {sp: np.zeros((ns, n), np.int64) for sp, n in self.ncell.items()}
        self.sb = self._enter(nc.sbuf_tensor("arena", [128, sbuf_bytes // 4], F32))
        self.ps = self._enter(nc.psum_tensor("psarena", [128, 4096], F32))
        self.sb_ptr = 0

    def _enter(self, cm):
        self._cm.append(cm)
        return cm.__enter__()

    def _new_stream(self, name):
        st = Stream(name, self._enter(self.nc.semaphore(name)), len(self.streams))
        self.streams.append(st)
        return st

    def close(self):
        for cm in reversed(self._cm):
            cm.__exit__(None, None, None)

    def alloc(self, shape, dtype, at=None):
        cnt = int(np.prod(shape))
        n = cnt * DTSZ[dtype]
        if at is None:
            at = self.sb_ptr
            self.sb_ptr = (at + n + 63) // 64 * 64
        assert at % 4 == 0 and at + n <= self.sbuf_bytes, (at, n, self.sbuf_bytes)
        ap = self.sb[:, at // 4:(at + n + 3) // 4]
        if dtype != F32:
            ap = ap.bitcast(dtype)[:, 0:cnt]
        if len(shape) > 1:
            names = " ".join(f"d{i}" for i in range(len(shape)))
            ap = ap.rearrange(f"p ({names}) -> p {names}", **{f"d{i}": s for i, s in enumerate(shape)})
        return T("sb", at, at + n, ap, list(shape), DTSZ[dtype])

    def psum(self, bank, width=512):
        lo = bank * 2048
        return T("ps", lo, lo + width * 4, self.ps[:, bank * 512: bank * 512 + width], [width], 4)

    @staticmethod
    def _cells(v):
        if v.space == "ps":
            return (v.lo // 2048) * (2048 // CELL), ((v.hi - 1) // 2048 + 1) * (2048 // CELL)
        return v.lo // CELL, (v.hi - 1) // CELL + 1

    def _deps(self, reads, writes, own=None):
        deps = {}
        for v in reads:
            a, b = self._cells(v)
            m = self.wv[v.space][:, a:b].max(axis=1)
            if v.space == "ps":
                m2 = self.rv[v.space][:, a:b].max(axis=1)
                if own is not None:
                    m2[own] = 0
                m = np.maximum(m, m2)
            for i in np.nonzero(m)[0]:
                deps[i] = max(deps.get(i, 0), int(m[i]))
        for v in writes:
            a, b = self._cells(v)
            m = np.maximum(self.wv[v.space][:, a:b].max(axis=1), self.rv[v.space][:, a:b].max(axis=1))
            for i in np.nonzero(m)[0]:
                deps[i] = max(deps.get(i, 0), int(m[i]))
        return deps

    def _record(self, st, val, reads, writes):
        for v in reads:
            a, b = self._cells(v)
            self.rv[v.space][st.idx, a:b] = val
        for v in writes:
            a, b = self._cells(v)
            self.wv[v.space][:, a:b] = 0
            self.rv[v.space][:, a:b] = 0
            self.wv[v.space][st.idx, a:b] = val

    def _waits(self, e, deps):
        waits = []
        for i, val in deps.items():
            st = self.streams[i]
            if st is e.stream and not e.self_sync:
                continue
            if e.seen.get(i, 0) >= val:
                continue
            e.seen[i] = val
            waits.append((st.sem, val))
        return waits

    @staticmethod
    def _flat(lst):
        out = []
        for r in lst:
            if r is None:
                continue
            if isinstance(r, (list, tuple)):
                out.extend(x for x in r if x is not None)
            else:
                out.append(r)
        return out

    def op(self, eng, fn, reads=(), writes=(), inc=True):
        e = self.eng[eng]
        reads, writes = self._flat(reads), self._flat(writes)
        waits = self._waits(e, self._deps(reads, writes, own=e.stream.idx))
        st = e.stream
        val = st.val + 1
        if inc:
            st.val = val
        e.pending_noinc = not inc
        e.ops.append((waits, _eager(fn), (st.sem, 1) if inc else None))
        self._record(st, val, reads, writes)

    def dma(self, queue, out, in_, reads=(), writes=()):
        e = self.eng[queue]
        reads, writes = self._flat(reads), self._flat(writes)
        kind = "sw" if queue == "pool" else "hw"
        slot = self.dma_slots[kind][self.dma_rr[kind]]
        self.dma_rr[kind] = (self.dma_rr[kind] + 1) % len(self.dma_slots[kind])
        deps = self._deps(reads, writes)
        if slot.val:
            deps[slot.idx] = max(deps.get(slot.idx, 0), slot.val)
        waits = self._waits(e, deps)
        slot.val += 16
        e.ops.append((waits, lambda h: h.dma_start(out=out, in_=in_), (slot.sem, 16)))
        self._record(slot, slot.val, reads, writes)

    def finish(self, queue="sp"):
        e = self.eng[queue]
        waits = [(st.sem, st.val) for st in self.streams
                 if st.val and e.seen.get(st.idx, 0) < st.val and st is not e.stream]
        e.ops.append((waits, None, None))

    def emit(self):
        for e in self.eng.values():
            assert not e.pending_noinc, e.name
        with self.nc.Block() as block:
            def run(e):
                def body(h):
                    for waits, fn, inc in e.ops:
                        for sem, val in waits:
                            h.wait_ge(sem, val)
                        if fn is not None:
                            ins = fn(h)
                            if inc is not None:
                                ins.then_inc(inc[0], inc[1])
                return body
            block.tensor(run(self.eng["pe"]))
            block.scalar(run(self.eng["act"]))
            block.vector(run(self.eng["dve"]))
            block.gpsimd(run(self.eng["pool"]))
            block.sync(run(self.eng["sp"]))


VEC_SPECS = [("ada_b0", 48), ("ada_b1", 48), ("ln1_g0", 8), ("ln1_b0", 8), ("ln2_g0", 8), ("ln2_b0", 8),
             ("ln1_g1", 8), ("ln1_b1", 8), ("ln2_g1", 8), ("ln2_b1", 8), ("b1_0", 32), ("b1_1", 32),
             ("b2_0", 8), ("b2_1", 8), ("pw1_b", 16), ("dw_w", 248), ("dw_b", 8), ("cn_g", 8), ("cn_b", 8),
             ("pw2_b", 8), ("es", 16), ("valid", 2), ("zero", 8)]
VOFF = {}
_o = 0
for _n, _c in VEC_SPECS:
    VOFF[_n] = _o
    _o += _c
NVEC = _o

PIECES = ([f"ada0_{j}" for j in range(6)] + [f"ada1_{j}" for j in range(6)] +
          ["wq", "wqs", "wk", "wv", "wo"] + [f"w1_0_{f}" for f in range(4)] + [f"w2_0_{f}" for f in range(4)] +
          ["pw1a", "pw1g", "pw2"] + [f"w1_1_{f}" for f in range(4)] + [f"w2_1_{f}" for f in range(4)])
PIDX = {n: i for i, n in enumerate(PIECES)}


def _mlp_order(l):
    o = []
    for f in range(4):
        o += [f"w1_{l}_{f}", f"w2_{l}_{f}"]
    return o


def build_program():
    nc = bass.Bass("TRN2", target_bir_lowering=False)
    dI = lambda n, s: nc.dram_tensor(n, s, F32, kind="ExternalInput").ap()
    dO = lambda n, s: nc.dram_tensor(n, s, F32, kind="ExternalOutput").ap()
    W = dI("W", [len(PIECES), 128, 8, 1024])
    xP = dI("xP", [128, 8, 1024])
    xS = dI("xS", [128, 8, 1056])
    xH = dI("xH", [128, 8, 512])
    cT = dI("cT", [128, 8, 2])
    vecs_d = dI("vecs", [128, NVEC])
    kctx_d = dI("kctx", [128, 4, 256])
    vctx_d = dI("vctx", [128, 2, 4, 64])
    rope_d = dI("rope", [128, 2, 1536])
    masks_d = dI("masks", [128, 4, 512])
    ident_d = dI("ident", [128, 128])
    yP = dO("yP", [128, 8, 1024])
    yS = dO("yS", [128, 8, 1024])
    kPo = dO("kP", [64, 4, 1024])
    vPo = dO("vP", [128, 8, 256])

    S = Sched(nc, 207 * 1024)
    op, dma = S.op, S.dma

    VEC = S.alloc([NVEC], F32)
    MODS = S.alloc([2, 48, 2], F32)
    COEF = S.alloc([4, 2, 5, 8], F32)
    CTMP = S.alloc([16], F32)
    EPSC = S.alloc([1], F32)
    CIN = S.alloc([8, 2], F32)
    SILUC = S.alloc([8, 2], BF16)
    ONES = S.alloc([128], BF16)
    MASKS = S.alloc([4, 512], BF16)
    IDENT = S.alloc([128], F32)
    ROPE = S.alloc([2, 1536], F32)
    RING = [S.alloc([8, 1024], BF16) for _ in range(4)]
    R = S.alloc([8, 1056], F32)
    H = S.alloc([8, 1056], BF16)
    XBASE = S.sb_ptr
    XEND = S.sbuf_bytes
    print("SBUF persistent bytes", XBASE, "scratch", XEND - XBASE)

    def vcol(name, i=0):
        o = VOFF[name] + i
        return VEC.ap[:, o:o + 1]

    def vrange(name, a, n):
        o = VOFF[name] + a
        return VEC.ap[:, o:o + n]

    bank_rr = {"mm": [0, [0, 1, 2, 3, 6, 7, 4, 5]], "st": [0, [4, 5]], "ln": [0, [4, 5, 6, 7]], "sc": [0, [6, 7, 0, 1, 2, 3, 4, 5]]}

    def bank(pool, width=512):
        st = bank_rr[pool]
        b = st[1][st[0] % len(st[1])]
        st[0] += 1
        return S.psum(b, width)

    ring_state = {"order": [], "next_load": 0, "slot_of": {}, "use": 0, "free": list(range(4))}

    def ring_extend(names):
        ring_state["order"].extend(names)

    def ring_issue():
        rs = ring_state
        i = rs["next_load"]
        name = rs["order"][i]
        si = rs["free"].pop(0)
        slot = RING[si]
        ncols = 256 if name == "wv" else (512 if name == "wkP" else 1024)
        p = PIDX["wk" if name == "wkP" else name]
        for hh in range(2):
            hv = V("sb", slot.lo + hh * 8192, slot.lo + (hh + 1) * 8192, None)
            dma("pool", slot.ap[:, hh * 4:(hh + 1) * 4, 0:ncols], W[p, :, hh * 4:(hh + 1) * 4, 0:ncols], writes=[hv])
        rs["slot_of"][i] = (slot, si)
        rs["next_load"] += 1

    def prefetch():
        rs = ring_state
        while rs["free"] and rs["next_load"] < len(rs["order"]):
            ring_issue()

    def need(name):
        rs = ring_state
        i = rs["use"]
        assert rs["order"][i] == name, (rs["order"][i], name)
        while rs["next_load"] <= i:
            assert rs["free"], ("ring full", name)
            ring_issue()
        rs["use"] += 1
        return rs["slot_of"][i][0]

    def release(slot):
        si = [k for k in range(4) if RING[k] is slot][0]
        assert si not in ring_state["free"]
        ring_state["free"].append(si)
        prefetch()

    dma("sp", VEC.ap, vecs_d, writes=[VEC])
    dma("sp", CIN.ap, cT, writes=[CIN])
    dma("sp", ROPE.ap, rope_d, writes=[ROPE])
    dma("pool", MASKS.ap, masks_d, writes=[MASKS])
    dma("sp", IDENT.ap, ident_d, writes=[IDENT])
    op("dve", lambda h: h.memset(EPSC.ap, EPS), writes=[EPSC])
    op("dve", lambda h: h.memset(ONES.ap, 1.0), writes=[ONES])
    op("act", lambda h: h.activation(SILUC.ap, CIN.ap, AF.Silu), reads=[CIN], writes=[SILUC])
    op("act", lambda h: h.activation(vrange("es", 0, 16), vrange("es", 0, 16), AF.Exp), reads=[VEC], writes=[VEC])

    def ada(l, blocks=range(6)):
        for j in blocks:
            slot = need(f"ada{l}_{j}")
            for m in range(8):
                ps = bank("mm", 2)
                for kc in range(8):
                    op("pe", lambda h, ps=ps, slot=slot, m=m, kc=kc: h.matmul(
                        ps.ap, slot.ap[:, kc, m * 128:(m + 1) * 128], SILUC.ap[:, kc, :], start=(kc == 0), stop=(kc == 7)),
                        reads=[slot, SILUC], writes=[ps], inc=(kc == 7))
                op("dve", lambda h, ps=ps, l=l, j=j, m=m: h.tensor_scalar(
                    MODS.ap[:, l, j * 8 + m, :], ps.ap, vcol(f"ada_b{l}", j * 8 + m), None, ALU.add),
                    reads=[ps, VEC], writes=[MODS])
            release(slot)

    def ada1_bufs():
        return [S.alloc([8, 512], BF16, at=XBASE + 43008 + i * 8192) for i in range(3)]

    def ada1_load(bufs, hp):
        j, half = hp // 2, hp % 2
        buf = bufs[hp % 3]
        dma("pool", buf.ap, W[PIDX[f"ada1_{j}"], :, :, half * 512:(half + 1) * 512], writes=[buf])

    def ada1_compute(bufs, hp):
        j, half = hp // 2, hp % 2
        buf = bufs[hp % 3]
        for m4 in range(4):
            m = half * 4 + m4
            ps = bank("mm", 2)
            for kc in range(8):
                op("pe", lambda h, ps=ps, m4=m4, kc=kc: h.matmul(
                    ps.ap, buf.ap[:, kc, m4 * 128:(m4 + 1) * 128], SILUC.ap[:, kc, :], start=(kc == 0), stop=(kc == 7)),
                    reads=[buf, SILUC], writes=[ps], inc=(kc == 7))
            op("dve", lambda h, ps=ps, j=j, m=m: h.tensor_scalar(
                MODS.ap[:, 1, j * 8 + m, :], ps.ap, vcol("ada_b1", j * 8 + m), None, ALU.add),
                reads=[ps, VEC], writes=[MODS])

    def mod(l, j, cond):
        return MODS.ap[:, l, j * 8:(j + 1) * 8, cond]

    def coef(sub, cond, l, jsh, jsc, jg, lng, lnb, bias, part="all"):
        C = lambda k: COEF.ap[:, sub, cond, k, :]
        t1 = CTMP.ap[:, 0:8]
        rd, wr = [MODS, VEC, COEF, CTMP], [COEF, CTMP]
        if part == "g":
            assert lng is None and bias is None
            op("dve", lambda h: h.tensor_copy(C(4), mod(l, jg, cond)), reads=rd, writes=wr)
            return
        op("dve", lambda h: h.tensor_scalar(t1, mod(l, jsc, cond), 1.0, None, ALU.add), reads=rd, writes=wr)
        if lng is None:
            op("dve", lambda h: h.tensor_copy(C(0), t1), reads=rd, writes=wr)
            op("dve", lambda h: h.tensor_copy(C(1), mod(l, jsh, cond)), reads=rd, writes=wr)
            op("dve", lambda h: h.memset(C(2), ALPHA), reads=rd, writes=wr)
            if bias is None:
                op("dve", lambda h: h.memset(C(3), 0.0), reads=rd, writes=wr)
            else:
                op("dve", lambda h: h.tensor_tensor(C(3), mod(l, jg, cond), vrange(bias, 0, 8), ALU.mult), reads=rd, writes=wr)
            if part == "h":
                return
        else:
            g, b = vrange(lng, 0, 8), vrange(lnb, 0, 8)
            op("dve", lambda h: h.tensor_tensor(C(0), t1, g, ALU.mult), reads=rd, writes=wr)
            op("dve", lambda h: h.tensor_tensor(C(1), t1, b, ALU.mult), reads=rd, writes=wr)
            op("dve", lambda h: h.tensor_tensor(C(1), C(1), mod(l, jsh, cond), ALU.add), reads=rd, writes=wr)
            op("dve", lambda h: h.tensor_scalar(C(2), g, ALPHA, None, ALU.mult), reads=rd, writes=wr)
            if bias is None:
                op("dve", lambda h: h.tensor_scalar(C(3), b, ALPHA, None, ALU.mult), reads=rd, writes=wr)
            else:
                op("dve", lambda h: h.tensor_tensor(C(3), mod(l, jg, cond), vrange(bias, 0, 8), ALU.mult), reads=rd, writes=wr)
                op("dve", lambda h: h.scalar_tensor_tensor(C(3), b, ALPHA, C(3), ALU.mult, ALU.add), reads=rd, writes=wr)
        op("dve", lambda h: h.tensor_copy(C(4), mod(l, jg, cond)), reads=rd, writes=wr)

    def cf(sub, cond, k, c):
        return COEF.ap[:, sub, cond, k, c:c + 1]

    def prep(sub, cond, a0, tw, src=None):
        for c in range(8):
            sv_ = R.v(c, a0, tw) if src is None else src.v(c, 0, tw)
            op("act", lambda h, c=c, sv_=sv_: h.activation(H.ap[:, c, a0:a0 + tw], sv_.ap, AF.Identity,
                                                           bias=cf(sub, cond, 1, c), scale=cf(sub, cond, 0, c)),
               reads=[sv_, COEF], writes=[H.v(c, a0, tw)])
        for c in range(8):
            sv_ = R.v(c, a0, tw) if src is None else src.v(c, 0, tw)
            if c % 4 != 3:
                op("dve", lambda h, c=c, sv_=sv_: h.tensor_scalar(R.ap[:, c, a0:a0 + tw], sv_.ap,
                                                                  cf(sub, cond, 2, c), cf(sub, cond, 3, c), ALU.mult, ALU.add),
                   reads=[sv_, COEF], writes=[R.v(c, a0, tw)])
            else:
                op("act", lambda h, c=c, sv_=sv_: h.activation(R.ap[:, c, a0:a0 + tw], sv_.ap, AF.Identity,
                                                               bias=cf(sub, cond, 3, c), scale=cf(sub, cond, 2, c)),
                   reads=[sv_, COEF], writes=[R.v(c, a0, tw)])

    def linear(slot, col0, nm, act, a0, tw, evac, nk=8, pool="mm", kbase=0):
        for m in range(nm):
            ps = bank(pool, tw)
            for kc in range(nk):
                op("pe", lambda h, ps=ps, m=m, kc=kc: h.matmul(
                    ps.ap, slot.ap[:, kbase + kc, col0 + m * 128: col0 + (m + 1) * 128], act.ap[:, kc, a0:a0 + tw],
                    start=(kc == 0), stop=(kc == nk - 1)),
                    reads=[slot, act.v(kc, a0, tw)], writes=[ps], inc=(kc == nk - 1))
            evac(m, ps)

    def ln_tmps(tw, tmp_at):
        ZB = S.alloc([8, tw], BF16, at=tmp_at)
        ZQ = S.alloc([8, tw], BF16, at=tmp_at + 8 * tw * 2)
        MEAN = S.alloc([tw], F32, at=tmp_at + 16 * tw * 2)
        RSTD = S.alloc([tw], F32, at=tmp_at + 16 * tw * 2 + tw * 4)
        return ZB, ZQ, MEAN, RSTD

    def ln_pre_chunk(src, c, a0, tw, tmp_at, copy_eng="act"):
        ZB, ZQ, _, _ = ln_tmps(tw, tmp_at)
        op("act", lambda h: h.activation(ZQ.ap[:, c, :], src.ap[:, c, a0:a0 + tw], AF.Square),
           reads=[src.v(c, a0, tw)], writes=[ZQ.v(c, 0, tw)])
        if copy_eng == "act":
            op("dve", lambda h: h.tensor_copy(ZB.ap[:, c, :], src.ap[:, c, a0:a0 + tw]),
               reads=[src.v(c, a0, tw)], writes=[ZB.v(c, 0, tw)])
        elif copy_eng == "actcopy":
            op("act", lambda h: h.activation(ZB.ap[:, c, :], src.ap[:, c, a0:a0 + tw], AF.Copy),
               reads=[src.v(c, a0, tw)], writes=[ZB.v(c, 0, tw)])
        else:
            op("pool", lambda h: h.tensor_copy(ZB.ap[:, c, :], src.ap[:, c, a0:a0 + tw]),
               reads=[src.v(c, a0, tw)], writes=[ZB.v(c, 0, tw)])

    def accum(sub, cond, a0, tw, ln_pre=False):
        def ev(m, ps):
            op("dve", lambda h: h.scalar_tensor_tensor(R.ap[:, m, a0:a0 + tw], ps.ap, cf(sub, cond, 4, m), R.ap[:, m, a0:a0 + tw],
                                                       ALU.mult, ALU.add),
               reads=[ps, COEF, R.v(m, a0, tw)], writes=[R.v(m, a0, tw)])
            if ln_pre:
                ln_pre_chunk(R, m, a0, tw, LN_TMP, copy_eng="actcopy")
        return ev

    def layer_norm(src, a0, tw, tmp_at, dst=None, copy_eng="act", pre_done=False):
        dst = src if dst is None else dst
        ZB, ZQ, MEAN, RSTD = ln_tmps(tw, tmp_at)
        if not pre_done:
            for c in range(8):
                ln_pre_chunk(src, c, a0, tw, tmp_at, copy_eng)
        p1, p2 = bank("ln", tw), bank("ln", tw)
        for c in range(8):
            op("pe", lambda h, c=c: h.matmul(p1.ap, ONES.ap, ZB.ap[:, c, :], start=(c == 0), stop=(c == 7)),
               reads=[ONES, ZB.v(c, 0, tw)], writes=[p1], inc=(c == 7))
        for c in range(8):
            op("pe", lambda h, c=c: h.matmul(p2.ap, ONES.ap, ZQ.ap[:, c, :], start=(c == 0), stop=(c == 7)),
               reads=[ONES, ZQ.v(c, 0, tw)], writes=[p2], inc=(c == 7))
        op("dve", lambda h: h.tensor_scalar(MEAN.ap, p1.ap, 1.0 / D, None, ALU.mult), reads=[p1], writes=[MEAN])
        op("dve", lambda h: h.tensor_tensor(RSTD.ap, MEAN.ap, MEAN.ap, ALU.mult), reads=[MEAN], writes=[RSTD])
        op("dve", lambda h: h.scalar_tensor_tensor(RSTD.ap, p2.ap, 1.0 / D, RSTD.ap, ALU.mult, ALU.subtract),
           reads=[p2, RSTD], writes=[RSTD])
        op("act", lambda h: h.activation(RSTD.ap, RSTD.ap, AF.Ln, bias=EPSC.ap[:, 0:1]), reads=[RSTD, EPSC], writes=[RSTD])
        op("act", lambda h: h.activation(RSTD.ap, RSTD.ap, AF.Exp, scale=-0.5), reads=[RSTD], writes=[RSTD])
        for c in range(8):
            op("dve", lambda h, c=c: h.tensor_tensor(dst.ap[:, c, a0:a0 + tw], src.ap[:, c, a0:a0 + tw], MEAN.ap, ALU.subtract),
               reads=[src.v(c, a0, tw), MEAN], writes=[dst.v(c, a0, tw)])
        for c in range(8):
            op("dve", lambda h, c=c: h.tensor_tensor(dst.ap[:, c, a0:a0 + tw], dst.ap[:, c, a0:a0 + tw], RSTD.ap, ALU.mult),
               reads=[dst.v(c, a0, tw), RSTD], writes=[dst.v(c, a0, tw)])

    def mlp(l, sub, cond, tiles, on_done=None, stage_hook=None):
        HID = [S.alloc([8, 512], BF16, at=XBASE + i * 8192) for i in range(2)]
        HT = [S.alloc([512], F32, at=XBASE + 36864 + i * 2048) for i in range(3)]
        ht_rr = [0]
        work = [(f, t) for f in range(4) for t in tiles]
        slots = {}

        def stage1(i):
            f, (a0, tw) = work[i]
            if stage_hook is not None:
                stage_hook(i)
            if f not in slots:
                slots[f] = (need(f"w1_{l}_{f}"), need(f"w2_{l}_{f}"))
            hid = HID[i % 2]

            def ev(m, ps):
                ht = HT[ht_rr[0] % 3]
                ht_rr[0] += 1
                op("act", lambda h: h.activation(ht.ap[:, 0:tw], ps.ap, AF.Relu, bias=vcol(f"b1_{l}", f * 8 + m)),
                   reads=[ps, VEC], writes=[ht])
                op("dve", lambda h: h.scalar_tensor_tensor(hid.ap[:, m, 0:tw], ps.ap, vcol(f"b1_{l}", f * 8 + m), ht.ap[:, 0:tw],
                                                           ALU.add, ALU.mult),
                   reads=[ps, VEC, ht], writes=[hid.v(m, 0, tw)])
            linear(slots[f][0], 0, 8, H, a0, tw, ev)
            if i + 1 == len(work) or work[i + 1][0] != f:
                release(slots[f][0])

        def stage2(i):
            f, (a0, tw) = work[i]
            hid = HID[i % 2]
            ev = accum(sub, cond, a0, tw, ln_pre=(f == 3))
            for m in range(8):
                ps = bank("mm", tw)
                for kc in range(8):
                    op("pe", lambda h, ps=ps, m=m, kc=kc: h.matmul(
                        ps.ap, slots[f][1].ap[:, kc, m * 128:(m + 1) * 128], hid.ap[:, kc, 0:tw], start=(kc == 0), stop=(kc == 7)),
                        reads=[slots[f][1], hid.v(kc, 0, tw)], writes=[ps], inc=(kc == 7))
                ev(m, ps)
            if i + 1 == len(work) or work[i + 1][0] != f:
                release(slots[f][1])
            if f == 3 and on_done is not None:
                on_done(work[i][1])

        for i in range(len(work)):
            stage1(i)
            if i > 0:
                stage2(i - 1)
        stage2(len(work) - 1)

    LN_TMP = XBASE + 16384

    def attn_bufs(nq, nkb, sphase):
        p = XBASE
        b = {}

        def A(name, shape, dt):
            nonlocal p
            b[name] = S.alloc(shape, dt, at=p)
            p = (p + int(np.prod(shape)) * DTSZ[dt] + 63) // 64 * 64
        A("Q", [8, nq], BF16)
        A("PT", [8, 512], BF16)
        A("REC", [1024], F32)
        A("KT", [4, nkb * 128], BF16)
        A("VA", [nkb, 4, 128], BF16)
        if sphase:
            A("RT", [4, 352], F32)
            A("KC", [4, 256], BF16)
            A("VC", [2, 4, 128], BF16)
        b["end"] = p
        return b

    pt_rr = [0]
    rt_rr = [0]
    rec_rr = [0]
    attn_pending = []
    GORD = [0, 2, 1, 3]

    scp_rr = [0]
    SC_PAIRS = [(0, 1), (2, 3), (6, 7)]

    def attn_core(B, j, qcol, nqw, ocol, ktiles):
        po = bank("st", 4 * nqw)
        nk = len(ktiles)
        ptvs = []

        def scores(ti):
            kfn, va_ap, va_v, mask = ktiles[ti]
            pi = pt_rr[0] % 8
            pt_rr[0] += 1
            ptv = B["PT"].v(pi, 0, 4 * nqw)
            ptvs.append(ptv)
            b0, b1 = SC_PAIRS[scp_rr[0] % 3]
            scp_rr[0] += 1
            pss = [S.psum(b0, 2 * nqw), S.psum(b1, 2 * nqw)]
            for half in range(2):
                ps = pss[half]
                k_ap, k_v = kfn(half)
                for gi in range(2):
                    hd = 4 * j + GORD[half * 2 + gi]
                    c = hd // 2
                    assert hd % 2 == half
                    op("pe", lambda h, ps=ps, gi=gi, c=c, half=half, k_ap=k_ap: h.matmul(
                        ps.ap[:, gi * nqw:(gi + 1) * nqw], k_ap, B["Q"].ap[half * 64:(half + 1) * 64, c, qcol:qcol + nqw],
                        start=True, stop=True),
                        reads=[k_v, B["Q"].v(c, qcol, nqw)], writes=[ps], inc=(gi == 1))
            src = S.ps[:, b0 * 512:(b0 + 2) * 512].rearrange("p (b n) -> p b n", b=2)[:, :, 0:2 * nqw]
            dst = ptv.ap.rearrange("p (b n) -> p b n", b=2)
            op("act", lambda h, src=src, dst=dst: h.activation(dst, src, AF.Exp, scale=SCALE), reads=pss, writes=[ptv])
            if mask is not None:
                mv, mq0 = mask
                m_ap = MASKS.ap[:, mv, :].rearrange("p (g q) -> p g q", g=4)[:, :, mq0:mq0 + nqw]
                p_ap = ptv.ap.rearrange("p (g q) -> p g q", g=4)
                op("dve", lambda h, p_ap=p_ap, m_ap=m_ap: h.tensor_tensor(p_ap, p_ap, m_ap, ALU.mult),
                   reads=[ptv, MASKS], writes=[ptv])

        def pv(ti):
            kfn, va_ap, va_v, mask = ktiles[ti]
            ptv = ptvs[ti]
            op("pe", lambda h, ptv=ptv, va_ap=va_ap, ti=ti: h.matmul(po.ap, va_ap, ptv.ap, start=(ti == 0), stop=(ti == nk - 1)),
               reads=[va_v, ptv], writes=[po], inc=(ti == nk - 1))

        for ti in range(min(3, nk)):
            scores(ti)
        for ti in range(nk):
            pv(ti)
            if ti + 3 < nk:
                scores(ti + 3)
        attn_pending.append(lambda: attn_norm(B, j, nqw, ocol, po))
        if len(attn_pending) > 1:
            attn_pending.pop(0)()

    def attn_flush():
        while attn_pending:
            attn_pending.pop(0)()

    def attn_norm(B, j, nqw, ocol, po):
        rec = B["REC"].cols((rec_rr[0] % 2) * 512, 4 * nqw)
        rec_rr[0] += 1
        eo = VOFF["es"] + 4 * j
        es_b = VEC.ap[64:128, eo:eo + 4].unsqueeze(2).to_broadcast([64, 4, nqw])
        r3 = rec.ap[64:128, :].rearrange("p (g q) -> p g q", g=4)
        op("dve", lambda h: h.tensor_tensor(r3, po.ap[64:128, :].rearrange("p (g q) -> p g q", g=4), es_b, ALU.add),
           reads=[po, VEC], writes=[rec])
        op("act", lambda h: h.activation(rec.ap[64:128, :], rec.ap[64:128, :], AF.Ln), reads=[rec], writes=[rec])
        op("act", lambda h: h.activation(rec.ap[64:128, :], rec.ap[64:128, :], AF.Exp, scale=-1.0), reads=[rec], writes=[rec])
        for half in range(2):
            sl = slice(half * 2 * nqw, (half + 1) * 2 * nqw)
            op("dve", lambda h, half=half, sl=sl: h.tensor_tensor(
                H.ap[half * 64:(half + 1) * 64, 2 * j:2 * j + 2, ocol:ocol + nqw],
                po.ap[0:64, sl].rearrange("p (g q) -> p g q", g=2),
                rec.ap[64:128, sl].rearrange("p (g q) -> p g q", g=2), ALU.mult),
                reads=[po, rec], writes=[H.v(2 * j, ocol, nqw), H.v(2 * j + 1, ocol, nqw)])

    def v_block(B, slot, src, scol, blk, kout=None):
        ps = bank("mm", 256)
        for kc in range(8):
            op("pe", lambda h, kc=kc: h.matmul(ps.ap, src.ap[:, kc, scol:scol + 128], slot.ap[:, kc, 0:256], start=(kc == 0), stop=(kc == 7)),
               reads=[slot, src.v(kc, scol, 128)], writes=[ps], inc=(kc == 7))
        va = B["VA"]
        lo = va.lo + blk * 4 * 128 * 2
        vav = V("sb", lo, lo + 4 * 128 * 2, va.ap[:, blk, :, 0:64])
        op("act", lambda h: h.activation(vav.ap, ps.ap.rearrange("p (j d) -> p j d", j=4), AF.Copy), reads=[ps], writes=[vav])
        if kout is not None:
            op("dve", lambda h: h.tensor_copy(kout.ap[:, blk, :], ps.ap), reads=[ps], writes=[kout.v(blk, 0, 256)])

    def va_view(B, blk, j):
        va = B["VA"]
        lo = va.lo + (blk * 4 + j) * 128 * 2
        return va.ap[:, blk, j, :], V("sb", lo, lo + 256, None)

    ring_extend(["ada0_0", "ada0_1", "wq", "wkP", "wv"] + [f"ada0_{j}" for j in range(2, 6)] + ["wo"] + _mlp_order(0) + ["pw1a", "pw1g", "pw2"] + _mlp_order(1))
    ring_extend(["wk", "wv", "wq", "wqs", "wo"] + _mlp_order(0) + ["pw1a", "pw1g", "pw2"] + _mlp_order(1))
    prefetch()
    ada(0, [0, 1])

    def conv_sublayer(cond, T_in, in_tiles, segs, out_tiles, ucols):
        ACC = S.alloc([8, 1024], F32, at=XBASE)
        lnp = XBASE + 32768
        p = XBASE + 43008
        UP = S.alloc([8, ucols], BF16, at=p)
        p = (p + 8 * ucols * 2 + 63) // 64 * 64
        SG = [S.alloc([512], F32, at=p + i * 2048) for i in range(2)]
        p += 4096
        DG = S.alloc([16, 128], BF16, at=p)
        p += 4096
        assert p <= XEND, (p, XEND)
        sa, sg_ = need("pw1a"), need("pw1g")
        op("dve", lambda h: h.memset(UP.ap, 0.0), writes=[UP])
        def upcol(col):
            for (c0, ln, u0) in segs:
                if c0 <= col < c0 + ln:
                    return u0 + (col - c0)
            raise AssertionError(col)
        sgi = [0]
        for (a0, tw) in in_tiles:
            parts = []
            x0 = a0
            while x0 < a0 + tw:
                for (c0, ln, u0) in segs:
                    if c0 <= x0 < c0 + ln:
                        e = min(a0 + tw, c0 + ln)
                        parts.append((x0, e - x0, u0 + x0 - c0))
                        x0 = e
                        break
                else:
                    raise AssertionError
            for m in range(8):
                psg = bank("mm", tw)
                for kc in range(8):
                    op("pe", lambda h, psg=psg, m=m, kc=kc: h.matmul(psg.ap, sg_.ap[:, kc, m * 128:(m + 1) * 128], H.ap[:, kc, a0:a0 + tw],
                                                                     start=(kc == 0), stop=(kc == 7)),
                       reads=[sg_, H.v(kc, a0, tw)], writes=[psg], inc=(kc == 7))
                sg = SG[sgi[0] % 2]
                sgi[0] += 1
                op("act", lambda h, psg=psg, sg=sg, m=m: h.activation(sg.ap[:, 0:tw], psg.ap, AF.Sigmoid, bias=vcol("pw1_b", 8 + m)),
                   reads=[psg, VEC], writes=[sg])
                psa = bank("mm", tw)
                for kc in range(8):
                    op("pe", lambda h, psa=psa, m=m, kc=kc: h.matmul(psa.ap, sa.ap[:, kc, m * 128:(m + 1) * 128], H.ap[:, kc, a0:a0 + tw],
                                                                     start=(kc == 0), stop=(kc == 7)),
                       reads=[sa, H.v(kc, a0, tw)], writes=[psa], inc=(kc == 7))
                for (x0, ln, u0) in parts:
                    op("dve", lambda h, psa=psa, sg=sg, m=m, x0=x0, ln=ln, u0=u0: h.scalar_tensor_tensor(
                        UP.ap[:, m, u0:u0 + ln], psa.ap[:, x0 - a0:x0 - a0 + ln], vcol("pw1_b", m), sg.ap[:, x0 - a0:x0 - a0 + ln],
                        ALU.add, ALU.mult),
                        reads=[psa, sg, VEC], writes=[UP.v(m, u0, ln)])
        release(sa); release(sg_)
        return UP, ACC, lnp, DG

    dg_rr = [0]

    def conv_taps(UP, ACC, lnp, DG, out_specs):
        for c in range(8):
            pss = [bank("sc", tw) for (_, tw, _) in out_specs]
            for w in range(31):
                di = dg_rr[0] % 16
                dg_rr[0] += 1
                dv = DG.v(di, 0, 128)
                if w % 2 == 0:
                    op("act", lambda h, dv=dv, w=w, c=c: h.activation(dv.ap, IDENT.ap, AF.Identity, scale=vcol("dw_w", w * 8 + c)),
                       reads=[IDENT, VEC], writes=[dv])
                else:
                    op("dve", lambda h, dv=dv, w=w, c=c: h.tensor_scalar(dv.ap, IDENT.ap, vcol("dw_w", w * 8 + c), None, ALU.mult),
                       reads=[IDENT, VEC], writes=[dv])
                for oi, (ps, (uc0, tw, d0)) in enumerate(zip(pss, out_specs)):
                    op("pe", lambda h, ps=ps, dv=dv, c=c, w=w, uc0=uc0, tw=tw: h.matmul(
                        ps.ap, dv.ap, UP.ap[:, c, uc0 - 15 + w:uc0 - 15 + w + tw], start=(w == 0), stop=(w == 30)),
                        reads=[dv, UP.v(c, uc0 - 15 + w, tw)], writes=[ps], inc=(w == 30 or oi == len(pss) - 1))
            for ps, (uc0, tw, d0) in zip(pss, out_specs):
                op("dve", lambda h, ps=ps, c=c, d0=d0, tw=tw: h.tensor_scalar(ACC.ap[:, c, d0:d0 + tw], ps.ap, vcol("dw_b", c), None, ALU.add),
                   reads=[ps, VEC], writes=[ACC.v(c, d0, tw)])

    def conv_post(ACC, lnp, own0):
        for i in range(4):
            d0 = i * 256
            layer_norm(ACC, d0, 256, lnp, copy_eng="pool")
            for c in range(8):
                op("act", lambda h, c=c, d0=d0: h.activation(H.ap[:, c, own0 + d0:own0 + d0 + 256], ACC.ap[:, c, d0:d0 + 256], AF.Silu,
                                                             bias=vcol("cn_b", c), scale=vcol("cn_g", c)),
                   reads=[ACC.v(c, d0, 256), VEC], writes=[H.v(c, own0 + d0, 256)])

    XSTAGE = [S.alloc([8, 352], F32, at=XBASE + 43008 + i * 11264) for i in range(2)]

    def run_phase(ph):
        cond = ph
        if ph == 0:
            T_, tiles, own0 = 1024, [(0, 512), (512, 512)], 0
            xin = xP
        else:
            T_, tiles, own0 = 1056, [(0, 352), (352, 352), (704, 352)], 16
            xin = xS
        own_tiles = [(own0, 512), (own0 + 512, 512)]
        xsrc = {}
        for ti, (a0, tw) in enumerate(tiles):
            if ph == 1 and ti < 2:
                xsrc[a0] = XSTAGE[ti]
            else:
                dma("sp", R.ap[:, :, a0:a0 + tw], xin[:, :, a0:a0 + tw], writes=R.vs(a0, tw))
        coef(0, cond, 0, 0, 1, 2, None, None, None, part=("h" if ph == 0 else "all"))
        B = attn_bufs(T_, 8 if ph == 0 else 12, ph == 1)
        if ph == 1:
            for (a0, tw) in tiles:
                prep(0, cond, a0, tw, src=xsrc.get(a0))
        op("pool", lambda h: h.memset(B["VA"].ap[:, :, :, 64:128], 1.0), writes=[B["VA"]])
        if ph == 1:
            op("pool", lambda h: h.memset(B["VC"].ap[:, :, :, 64:128], 1.0), writes=[B["VC"]])
        if ph == 0:
            KOUT = S.alloc([4, 1024], F32, at=B["end"])
            VOUT = S.alloc([8, 256], F32, at=B["end"] + 16384)
            assert B["end"] + 16384 + 8192 <= XEND
            for (a0, tw) in tiles:
                prep(0, cond, a0, tw)
            sq, sk, sv = need("wq"), need("wkP"), need("wv")
            for (a0, tw) in tiles:
                def evq(m, ps, a0=a0, tw=tw):
                    op("act", lambda h: h.activation(B["Q"].ap[:, m, a0:a0 + tw], ps.ap, AF.Copy), reads=[ps], writes=[B["Q"].v(m, a0, tw)])
                linear(sq, 0, 8, H, a0, tw, evq)

                def evk(m, ps, a0=a0, tw=tw):
                    op("act", lambda h: h.activation(B["KT"].ap[:, m, a0:a0 + tw], ps.ap, AF.Copy), reads=[ps], writes=[B["KT"].v(m, a0, tw)])
                    op("dve", lambda h: h.tensor_copy(KOUT.ap[0:64, m, a0:a0 + tw], ps.ap[0:64, :]), reads=[ps], writes=[KOUT.v(m, a0, tw)])
                linear(sk, 0, 4, H, a0, tw, evk)
                for blk in range(a0 // 128, (a0 + tw) // 128):
                    v_block(B, sv, H, blk * 128, blk, kout=VOUT)
            release(sq); release(sk); release(sv)
            dma("sp", kPo, KOUT.ap[0:64, :, :], reads=[KOUT])
            dma("sp", vPo, VOUT.ap, reads=[VOUT])
            for s in range(4):
                ada(0, [2 + s])
                for j in range(4):
                    kts = []
                    for kt in range(2):
                        col = s * 256 + kt * 128
                        kfn = (lambda half, col=col, j=j: (B["KT"].ap[half * 64:(half + 1) * 64, j, col:col + 128], B["KT"].v(j, col, 128)))
                        va_ap, va_v = va_view(B, s * 2 + kt, j)
                        kts.append((kfn, va_ap, va_v, None))
                    for qh in range(2):
                        attn_core(B, j, s * 256 + qh * 128, 128, s * 256 + qh * 128, kts)
            coef(0, cond, 0, 0, 1, 2, None, None, None, part="g")
        else:
            XHT = S.alloc([8, 512], F32, at=XBASE)
            HH = S.alloc([8, 512], BF16, at=XBASE + 16384)
            dma("sp", XHT.ap, xH, writes=[XHT])
            for c in range(8):
                op("act", lambda h, c=c: h.activation(HH.ap[:, c, :], XHT.ap[:, c, :], AF.Identity, bias=cf(0, cond, 1, c), scale=cf(0, cond, 0, c)),
                   reads=[XHT.v(c, 0, 512), COEF], writes=[HH.v(c, 0, 512)])
            sk, sv = need("wk"), need("wv")
            dma("pool", B["KC"].ap, kctx_d, writes=[B["KC"]])
            dma("pool", B["VC"].ap[:, :, :, 0:64], vctx_d, writes=[B["VC"]])
            RT = B["RT"]

            def rope_evac(dst, m, ps_a, ps_b, dcol, tw, rcol):
                rb = (rt_rr[0] % 2) * 2
                rt_rr[0] += 1
                t1, t2 = RT.v(rb, 0, tw), RT.v(rb + 1, 0, tw)
                op("dve", lambda h: h.tensor_tensor(t1.ap, ps_a.ap, ROPE.ap[:, 0, rcol:rcol + tw], ALU.mult), reads=[ps_a, ROPE], writes=[t1])
                op("dve", lambda h: h.tensor_tensor(t2.ap, ps_b.ap, ROPE.ap[:, 1, rcol:rcol + tw], ALU.mult), reads=[ps_b, ROPE], writes=[t2])
                op("pool", lambda h: h.tensor_tensor(dst.ap[:, m, dcol:dcol + tw], t1.ap, t2.ap, ALU.add), reads=[t1, t2], writes=[dst.v(m, dcol, tw)])

            def k_proj(src, scol, tw, ecol):
                for m in range(4):
                    pa, pb = bank("mm", tw), bank("mm", tw)
                    for (pp, cbase) in ((pa, 0), (pb, 512)):
                        for kc in range(8):
                            op("pe", lambda h, pp=pp, cbase=cbase, m=m, kc=kc: h.matmul(
                                pp.ap, sk.ap[:, kc, cbase + m * 128:cbase + (m + 1) * 128], src.ap[:, kc, scol:scol + tw],
                                start=(kc == 0), stop=(kc == 7)),
                                reads=[sk, src.v(kc, scol, tw)], writes=[pp], inc=(kc == 7))
                    rope_evac(B["KT"], m, pa, pb, ecol, tw, ecol)

            k_proj(HH, 0, 256, 0)
            k_proj(HH, 256, 256, 1280)
            for i, blk in enumerate((0, 1, 10, 11)):
                v_block(B, sv, HH, i * 128, blk)
            for (a0, tw) in tiles:
                k_proj(H, a0, tw, 240 + a0)
            for blk in range(2, 10):
                v_block(B, sv, H, 16 + (blk - 2) * 128, blk)
            sq, sqs = need("wq"), need("wqs")
            for (a0, tw) in tiles:
                for m in range(8):
                    pa, pb = bank("mm", tw), bank("mm", tw)
                    for (pp, sl) in ((pa, sq), (pb, sqs)):
                        for kc in range(8):
                            op("pe", lambda h, pp=pp, sl=sl, m=m, kc=kc: h.matmul(
                                pp.ap, sl.ap[:, kc, m * 128:(m + 1) * 128], H.ap[:, kc, a0:a0 + tw], start=(kc == 0), stop=(kc == 7)),
                                reads=[sl, H.v(kc, a0, tw)], writes=[pp], inc=(kc == 7))
                    rope_evac(B["Q"], m, pa, pb, a0, tw, 240 + a0)
            release(sk); release(sv); release(sq); release(sqs)
            def ctx_tiles(j):
                out = []
                for kt in range(2):
                    kfn = (lambda half, kt=kt, j=j: (B["KC"].ap[half * 64:(half + 1) * 64, j, kt * 128:(kt + 1) * 128], B["KC"].v(j, kt * 128, 128)))
                    vc = B["VC"]
                    lo = vc.lo + (kt * 4 + j) * 256
                    out.append((kfn, vc.ap[:, kt, j, :], V("sb", lo, lo + 256, None), None))
                return out

            def loc_tile(e, j, mask):
                kfn = (lambda half, e=e, j=j: (B["KT"].ap[half * 64:(half + 1) * 64, j, e * 128:(e + 1) * 128], B["KT"].v(j, e * 128, 128)))
                va_ap, va_v = va_view(B, e, j)
                return (kfn, va_ap, va_v, mask)

            qblocks = [(1, 0, 16, 112)] + [(n + 2, 16 + n * 128, 128, 0) for n in range(8)] + [(10, 1040, 16, 0)]
            for (e, qcol, nqw, mq0) in qblocks:
                for j in range(4):
                    mprev = 2 if e == 2 else 0
                    mnext = 3 if e == 9 else 1
                    kts = [loc_tile(e - 1, j, (mprev, mq0)), loc_tile(e, j, None), loc_tile(e + 1, j, (mnext, mq0))] + ctx_tiles(j)
                    attn_core(B, j, qcol, nqw, qcol, kts)
        attn_flush()
        def fin(sub_next, t):
            layer_norm(R, t[0], t[1], LN_TMP, pre_done=True)
            prep(sub_next, cond, t[0], t[1])

        coef(1, cond, 0, 3, 4, 5, "ln1_g0", "ln1_b0", "b2_0")
        so = need("wo")
        for i, (a0, tw) in enumerate(tiles):
            linear(so, 0, 8, H, a0, tw, accum(0, cond, a0, tw, ln_pre=True))
            if i == len(tiles) - 1:
                release(so)
            fin(1, tiles[i])
        def coefs_l1():
            coef(2, cond, 1, 0, 1, 2, "ln2_g0", "ln2_b0", "pw2_b")
            coef(3, cond, 1, 3, 4, 5, "ln1_g1", "ln1_b1", "b2_1")

        if ph == 0:
            abufs = ada1_bufs()

            def hook(i):
                if 1 <= i <= 6:
                    ada1_compute(abufs, 2 * (i - 1))
                    ada1_compute(abufs, 2 * (i - 1) + 1)
                if i <= 5:
                    ada1_load(abufs, 2 * i)
                    ada1_load(abufs, 2 * i + 1)
                if i == 7:
                    coefs_l1()
            assert len(tiles) * 4 == 8
            mlp(0, 1, cond, tiles, on_done=lambda t: fin(2, t), stage_hook=hook)
        else:
            coefs_l1()
            mlp(0, 1, cond, tiles, on_done=lambda t: fin(2, t))
        if ph == 0:
            segs = [(s * 256, 256, s * 286 + 15) for s in range(4)]
            ucols = 4 * 286
        else:
            segs = [(0, 1056, 0)]
            ucols = 1056
        UP, ACC, lnp, DG = conv_sublayer(cond, T_, tiles, segs, None, ucols)
        if ph == 1:
            for c in range(8):
                op("dve", lambda h, c=c: h.tensor_scalar(UP.ap[:, c, 0:16], UP.ap[:, c, 0:16], vcol("valid", 0), None, ALU.mult),
                   reads=[UP.v(c, 0, 16), VEC], writes=[UP.v(c, 0, 16)])
                op("dve", lambda h, c=c: h.tensor_scalar(UP.ap[:, c, 1040:1056], UP.ap[:, c, 1040:1056], vcol("valid", 1), None, ALU.mult),
                   reads=[UP.v(c, 1040, 16), VEC], writes=[UP.v(c, 1040, 16)])
            out_specs = [(16 + i * 512, 512, i * 512) for i in range(2)]
        else:
            out_specs = [(s * 286 + 15, 256, s * 256) for s in range(4)]
        conv_taps(UP, ACC, lnp, DG, out_specs)
        conv_post(ACC, lnp, own0)
        if ph == 0:
            for ti in range(2):
                dma("sp", XSTAGE[ti].ap, xS[:, :, ti * 352:(ti + 1) * 352], writes=[XSTAGE[ti]])
        sp2 = need("pw2")
        for i, (a0, tw) in enumerate(own_tiles):
            linear(sp2, 0, 8, H, a0, tw, accum(2, cond, a0, tw, ln_pre=True))
            if i == len(own_tiles) - 1:
                release(sp2)
            fin(3, own_tiles[i])
        yout = yP if ph == 0 else yS

        def final(t):
            a0, tw = t
            layer_norm(R, a0, tw, LN_TMP, pre_done=True)
            for c in range(8):
                op("act", lambda h, c=c: h.activation(R.ap[:, c, a0:a0 + tw], R.ap[:, c, a0:a0 + tw], AF.Identity,
                                                      bias=vcol("ln2_b1", c), scale=vcol("ln2_g1", c)),
                   reads=[R.v(c, a0, tw), VEC], writes=[R.v(c, a0, tw)])
                if c % 4 == 3:
                    c0 = c - 3
                    dma("sp", yout[:, c0:c0 + 4, a0 - own0:a0 - own0 + tw], R.ap[:, c0:c0 + 4, a0:a0 + tw],
                        reads=[R.v(cc, a0, tw) for cc in range(c0, c0 + 4)])

        mlp(1, 3, cond, own_tiles, on_done=final)

    run_phase(0)
    run_phase(1)
    S.finish("sp")
    S.emit()
    S.close()
    return nc


def _fm(v):
    v = np.asarray(v, np.float32).reshape(-1, 128)
    return np.ascontiguousarray(v.T)


def _wpiece(w):
    out = np.zeros((128, 8, 1024), np.float32)
    out[:, :, :w.shape[1]] = w.reshape(8, 128, w.shape[1]).transpose(1, 0, 2)
    return out


_NC_CACHE = {}


def kernel(x_prompt, x_sample, cache_k, cache_v, c, c_ctx, ada_w, ada_b, attn_w_qkv, attn_w_o, attn_sink,
           conv_pw1_w, conv_pw1_b, conv_dw_w, conv_dw_b, conv_norm_g, conv_norm_b, conv_pw2_w, conv_pw2_b,
           ln1_g, ln1_b, mlp_w1, mlp_b1, mlp_w2, mlp_b2, ln2_g, ln2_b):
    f = lambda a: np.asarray(a, np.float32)
    x_prompt, x_sample, cache_k, cache_v, c, c_ctx = map(f, (x_prompt, x_sample, cache_k, cache_v, c, c_ctx))
    ada_w, ada_b, wqkv, wo = f(ada_w), f(ada_b), f(attn_w_qkv)[0], f(attn_w_o)[0]
    Wall = np.zeros((len(PIECES), 128, 8, 1024), np.float32)
    for l in range(2):
        for j in range(6):
            Wall[PIDX[f"ada{l}_{j}"]] = _wpiece(ada_w[l][:, j * 1024:(j + 1) * 1024])
        for fb in range(4):
            Wall[PIDX[f"w1_{l}_{fb}"]] = _wpiece(f(mlp_w1)[l][:, fb * 1024:(fb + 1) * 1024])
            Wall[PIDX[f"w2_{l}_{fb}"]] = _wpiece(f(mlp_w2)[l][fb * 1024:(fb + 1) * 1024, :])
    partner = np.concatenate([np.arange(16, 32), np.arange(0, 16), np.arange(48, 64), np.arange(32, 48)])
    wq = wqkv[:, 0:1024]
    wk = wqkv[:, 1024:1280]
    wv = wqkv[:, 1280:1536]
    qperm = (np.arange(16)[:, None] * 64 + partner[None, :]).reshape(-1)
    Wall[PIDX["wq"]] = _wpiece(wq)
    Wall[PIDX["wqs"]] = _wpiece(wq[:, qperm])
    kd = np.concatenate([np.concatenate([wk[:, j * 64:(j + 1) * 64]] * 2, 1) for j in range(4)], 1)
    ks = np.concatenate([np.concatenate([wk[:, j * 64 + partner]] * 2, 1) for j in range(4)], 1)
    Wall[PIDX["wk"]] = _wpiece(np.concatenate([kd, ks], 1))
    Wall[PIDX["wv"]] = _wpiece(wv)
    Wall[PIDX["wo"]] = _wpiece(wo)
    pw1 = f(conv_pw1_w)[0]
    Wall[PIDX["pw1a"]] = _wpiece(pw1[:, 0:1024])
    Wall[PIDX["pw1g"]] = _wpiece(pw1[:, 1024:2048])
    Wall[PIDX["pw2"]] = _wpiece(f(conv_pw2_w)[0])
    vec = np.zeros((128, NVEC), np.float32)

    def put(name, arr):
        arr = np.asarray(arr, np.float32)
        vec[:, VOFF[name]:VOFF[name] + arr.shape[1]] = arr
    for l in range(2):
        put(f"ada_b{l}", _fm(ada_b[l]))
        put(f"ln1_g{l}", _fm(f(ln1_g)[l])); put(f"ln1_b{l}", _fm(f(ln1_b)[l]))
        put(f"ln2_g{l}", _fm(f(ln2_g)[l])); put(f"ln2_b{l}", _fm(f(ln2_b)[l]))
        put(f"b1_{l}", _fm(f(mlp_b1)[l])); put(f"b2_{l}", _fm(f(mlp_b2)[l]))
    put("pw1_b", _fm(f(conv_pw1_b)[0]))
    put("dw_w", _fm(f(conv_dw_w)[0].reshape(-1)))
    put("dw_b", _fm(f(conv_dw_b)[0])); put("cn_g", _fm(f(conv_norm_g)[0])); put("cn_b", _fm(f(conv_norm_b)[0]))
    put("pw2_b", _fm(f(conv_pw2_b)[0]))
    gperm = np.array([4 * j + g for j in range(4) for g in (0, 2, 1, 3)])
    put("es", np.broadcast_to(f(attn_sink)[0][gperm][None, :], (128, 16)))
    half = 32
    inv = (10000.0 ** (-np.arange(0, half, 2, dtype=np.float32) / half)).astype(np.float32)
    ki = np.arange(128)[:, None]
    qi = np.arange(128)[None, :]
    tri_prev = (ki >= qi).astype(np.float32)
    tri_next = (ki <= qi).astype(np.float32)
    in_maps = []
    for core in range(8):
        b, s0 = core // 4, (core % 4) * 1024
        xp = x_prompt[4 * core:4 * core + 4].reshape(1024, 8, 128).transpose(2, 1, 0)

        def tok(lo, hi):
            out = np.zeros((hi - lo, 1024), np.float32)
            a, e = max(lo, 0), min(hi, 4096)
            out[a - lo:e - lo] = x_sample[b, a:e]
            return out
        xs = tok(s0 - 16, s0 + 1040).reshape(1056, 8, 128).transpose(2, 1, 0)
        xh = np.concatenate([tok(s0 - 256, s0), tok(s0 + 1024, s0 + 1280)], 0).reshape(512, 8, 128).transpose(2, 1, 0)
        ct = np.stack([c_ctx, c[b]], -1).reshape(8, 128, 2).transpose(1, 0, 2)
        v = vec.copy()
        vl, vr = float(s0 > 0), float(s0 + 1024 < 4096)
        v[:, VOFF["valid"]] = vl
        v[:, VOFF["valid"] + 1] = vr
        kc_ = cache_k[b, 0].transpose(2, 1, 0)
        kctx = np.concatenate([kc_, kc_], 0)
        vctx = cache_v[b, 0].reshape(2, 128, 4, 64).transpose(1, 0, 2, 3)
        pos = np.arange(s0 - 256, s0 + 1280)
        row = (pos // 64).astype(np.float32)
        col = (pos % 64).astype(np.float32)
        ang = np.concatenate([row[None, :] * inv[:, None]] * 2 + [col[None, :] * inv[:, None]] * 2, 0)
        cos = np.cos(ang).astype(np.float32)
        sin = np.sin(ang).astype(np.float32)
        sgn = np.concatenate([-np.ones(16), np.ones(16), -np.ones(16), np.ones(16)]).astype(np.float32)[:, None]
        rope = np.stack([np.concatenate([cos, cos], 0), np.concatenate([sin * sgn, sin * sgn], 0)], 1)
        mk = np.stack([np.tile(tri_prev, (1, 4)), np.tile(tri_next, (1, 4)),
                       np.tile(tri_prev, (1, 4)) * vl, np.tile(tri_next, (1, 4)) * vr], 1)
        in_maps.append({"W": Wall, "xP": np.ascontiguousarray(xp), "xS": np.ascontiguousarray(xs), "xH": np.ascontiguousarray(xh),
                        "cT": np.ascontiguousarray(ct), "vecs": v, "kctx": np.ascontiguousarray(kctx),
                        "vctx": np.ascontiguousarray(vctx), "rope": np.ascontiguousarray(rope.astype(np.float32)),
                        "masks": np.ascontiguousarray(mk.astype(np.float32)), "ident": np.eye(128, dtype=np.float32)})
    if "nc" not in _NC_CACHE:
        _NC_CACHE["nc"] = build_program()
    nc = _NC_CACHE["nc"]
    res = run_bass_kernel_spmd(nc, in_maps, core_ids=list(range(8)))
    y_prompt = np.zeros((32, 256, 1024), np.float32)
    y_sample = np.zeros((2, 4096, 1024), np.float32)
    new_k = np.zeros((32, 1, 256, 4, 64), np.float32)
    new_v = np.zeros((32, 1, 256, 4, 64), np.float32)
    for core in range(8):
        r = res.results[core]
        b, s0 = core // 4, (core % 4) * 1024
        y_prompt[4 * core:4 * core + 4] = r["yP"].transpose(2, 1, 0).reshape(4, 256, 1024)
        y_sample[b, s0:s0 + 1024] = r["yS"].transpose(2, 1, 0).reshape(1024, 1024)
        new_k[4 * core:4 * core + 4, 0] = r["kP"].transpose(2, 1, 0).reshape(4, 256, 4, 64)
        new_v[4 * core:4 * core + 4, 0] = r["vP"].transpose(1, 0, 2).reshape(4, 256, 4, 64)
    return (y_prompt, y_sample, new_k, new_v)
```

```python
import numpy as np
import concourse.bass as bass
import concourse.mybir as mybir
from concourse.bass_utils import run_bass_kernel_spmd

F32 = mybir.dt.float32
BF16 = mybir.dt.bfloat16
AF = mybir.ActivationFunctionType
ALU = mybir.AluOpType
DTSZ = {F32: 4, BF16: 2}
CELL = 32

D = 1024
NCH = 8
ALPHA = 4.0 ** 0.25
EPS = 1e-5
SCALE = 0.125


class V:
    def __init__(self, space, lo, hi, ap):
        self.space, self.lo, self.hi, self.ap = space, lo, hi, ap


class T(V):
    def __init__(self, space, lo, hi, ap, shape, esz):
        super().__init__(space, lo, hi, ap)
        self.shape, self.esz = shape, esz

    def v(self, c, a0, w):
        n = self.shape[-1]
        lo = self.lo + (c * n + a0) * self.esz
        return V(self.space, lo, lo + w * self.esz, self.ap[:, c, a0:a0 + w])

    def vs(self, a0, w):
        return [self.v(c, a0, w) for c in range(self.shape[0])]

    def cols(self, a0, w):
        lo = self.lo + a0 * self.esz
        return V(self.space, lo, lo + w * self.esz, self.ap[:, a0:a0 + w])


class _Rec:
    def __init__(self):
        self.call = None

    def __getattr__(self, name):
        def f(*a, **k):
            self.call = (name, a, k)
        return f


def _eager(fn):
    r = _Rec()
    fn(r)
    name, a, k = r.call
    return lambda h: getattr(h, name)(*a, **k)


class Stream:
    def __init__(self, name, sem, idx):
        self.name, self.sem, self.idx, self.val = name, sem, idx, 0


class Engine:
    def __init__(self, name, stream, self_sync):
        self.name, self.stream, self.self_sync = name, stream, self_sync
        self.ops, self.seen, self.pending_noinc = [], {}, False


class Sched:
    def __init__(self, nc, sbuf_bytes, n_dma_slots=10):
        self.nc, self._cm, self.streams, self.eng = nc, [], [], {}
        for name, ss in (("pe", False), ("act", True), ("dve", True), ("pool", True), ("sp", False)):
            self.eng[name] = Engine(name, self._new_stream(name), ss)
        self.dma_slots = {"hw": [self._new_stream(f"dma{i}") for i in range(n_dma_slots)],
                          "sw": [self._new_stream(f"swdma{i}") for i in range(n_dma_slots)]}
        self.dma_rr = {"hw": 0, "sw": 0}
        ns = len(self.streams)
        self.sbuf_bytes = sbuf_bytes
        self.ncell = {"sb": sbuf_bytes // CELL + 1, "ps": 16384 // CELL}
        self.wv = {sp: np.zeros((ns, n), np.int64) for sp, n in self.ncell.items()}
        self.rv = {sp: np.zeros((ns, n), np.int64) for sp, n in self.ncell.items()}
        self.sb = self._enter(nc.sbuf_tensor("arena", [128, sbuf_bytes // 4], F32))
        self.ps = self._enter(nc.psum_tensor("psarena", [128, 4096], F32))
        self.sb_ptr = 0

    def _enter(self, cm):
        self._cm.append(cm)
        return cm.__enter__()

    def _new_stream(self, name):
        st = Stream(name, self._enter(self.nc.semaphore(name)), len(self.streams))
        self.streams.append(st)
        return st

    def close(self):
        for cm in reversed(self._cm):
            cm.__exit__(None, None, None)

    def alloc(self, shape, dtype, at=None):
        cnt = int(np.prod(shape))
        n = cnt * DTSZ[dtype]
        if at is None:
            at = self.sb_ptr
            self.sb_ptr = (at + n + 63) // 64 * 64
        assert at % 4 == 0 and at + n <= self.sbuf_bytes, (at, n, self.sbuf_bytes)
        ap = self.sb[:, at // 4:(at + n + 3) // 4]
        if dtype != F32:
            ap = ap.bitcast(dtype)[:, 0:cnt]
        if len(shape) > 1:
            names = " ".join(f"d{i}" for i in range(len(shape)))
            ap = ap.rearrange(f"p ({names}) -> p {names}", **{f"d{i}": s for i, s in enumerate(shape)})
        return T("sb", at, at + n, ap, list(shape), DTSZ[dtype])

    def psum(self, bank, width=512):
        lo = bank * 2048
        return T("ps", lo, lo + width * 4, self.ps[:, bank * 512: bank * 512 + width], [width], 4)

    @staticmethod
    def _cells(v):
        if v.space == "ps":
            return (v.lo // 2048) * (2048 // CELL), ((v.hi - 1) // 2048 + 1) * (2048 // CELL)
        return v.lo // CELL, (v.hi - 1) // CELL + 1

    def _deps(self, reads, writes, own=None):
        deps = {}
        for v in reads:
            a, b = self._cells(v)
            m = self.wv[v.space][:, a:b].max(axis=1)
            if v.space == "ps":
                m2 = self.rv[v.space][:, a:b].max(axis=1)
                if own is not None:
                    m2[own] = 0
                m = np.maximum(m, m2)
            for i in np.nonzero(m)[0]:
                deps[i] = max(deps.get(i, 0), int(m[i]))
        for v in writes:
            a, b = self._cells(v)
            m = np.maximum(self.wv[v.space][:, a:b].max(axis=1), self.rv[v.space][:, a:b].max(axis=1))
            for i in np.nonzero(m)[0]:
                deps[i] = max(deps.get(i, 0), int(m[i]))
        return deps

    def _record(self, st, val, reads, writes):
        for v in reads:
            a, b = self._cells(v)
            self.rv[v.space][st.idx, a:b] = val
        for v in writes:
            a, b = self._cells(v)
            self.wv[v.space][:, a:b] = 0
            self.rv[v.space][:, a:b] = 0
            self.wv[v.space][st.idx, a:b] = val

    def _waits(self, e, deps):
        waits = []
        for i, val in deps.items():
            st = self.streams[i]
            if st is e.stream and not e.self_sync:
                continue
            if e.seen.get(i, 0) >= val:
                continue
            e.seen[i] = val
            waits.append((st.sem, val))
        return waits

    @staticmethod
    def _flat(lst):
        out = []
        for r in lst:
            if r is None:
                continue
            if isinstance(r, (list, tuple)):
                out.extend(x for x in r if x is not None)
            else:
                out.append(r)
        return out

    def op(self, eng, fn, reads=(), writes=(), inc=True):
        e = self.eng[eng]
        reads, writes = self._flat(reads), self._flat(writes)
        waits = self._waits(e, self._deps(reads, writes, own=e.stream.idx))
        st = e.stream
        val = st.val + 1
        if inc:
            st.val = val
        e.pending_noinc = not inc
        e.ops.append((waits, _eager(fn), (st.sem, 1) if inc else None))
        self._record(st, val, reads, writes)

    def dma(self, queue, out, in_, reads=(), writes=()):
        e = self.eng[queue]
        reads, writes = self._flat(reads), self._flat(writes)
        kind = "sw" if queue == "pool" else "hw"
        slot = self.dma_slots[kind][self.dma_rr[kind]]
        self.dma_rr[kind] = (self.dma_rr[kind] + 1) % len(self.dma_slots[kind])
        deps = self._deps(reads, writes)
        if slot.val:
            deps[slot.idx] = max(deps.get(slot.idx, 0), slot.val)
        waits = self._waits(e, deps)
        slot.val += 16
        e.ops.append((waits, lambda h: h.dma_start(out=out, in_=in_), (slot.sem, 16)))
        self._record(slot, slot.val, reads, writes)

    def finish(self, queue="sp"):
        e = self.eng[queue]
        waits = [(st.sem, st.val) for st in self.streams
                 if st.val and e.seen.get(st.idx, 0) < st.val and st is not e.stream]
        e.ops.append((waits, None, None))

    def emit(self):
        for e in self.eng.values():
            assert not e.pending_noinc, e.name
        with self.nc.Block() as block:
            def run(e):
                def body(h):
                    for waits, fn, inc in e.ops:
                        for sem, val in waits:
                            h.wait_ge(sem, val)
                        if fn is not None:
                            ins = fn(h)
                            if inc is not None:
                                ins.then_inc(inc[0], inc[1])
                return body
            block.tensor(run(self.eng["pe"]))
            block.scalar(run(self.eng["act"]))
            block.vector(run(self.eng["dve"]))
            block.gpsimd(run(self.eng["pool"]))
            block.sync(run(self.eng["sp"]))


VEC_SPECS = [("ada_b0", 48), ("ada_b1", 48), ("ln1_g0", 8), ("ln1_b0", 8), ("ln2_g0", 8), ("ln2_b0", 8),
             ("ln1_g1", 8), ("ln1_b1", 8), ("ln2_g1", 8), ("ln2_b1", 8), ("b1_0", 32), ("b1_1", 32),
             ("b2_0", 8), ("b2_1", 8), ("pw1_b", 16), ("dw_w", 248), ("dw_b", 8), ("cn_g", 8), ("cn_b", 8),
             ("pw2_b", 8), ("es", 16), ("valid", 2), ("zero", 8)]
VOFF = {}
_o = 0
for _n, _c in VEC_SPECS:
    VOFF[_n] = _o
    _o += _c
NVEC = _o

PIECES = ([f"ada0_{j}" for j in range(6)] + [f"ada1_{j}" for j in range(6)] +
          ["wq", "wqs", "wk", "wv", "wo"] + [f"w1_0_{f}" for f in range(4)] + [f"w2_0_{f}" for f in range(4)] +
          ["pw1a", "pw1g", "pw2"] + [f"w1_1_{f}" for f in range(4)] + [f"w2_1_{f}" for f in range(4)])
PIDX = {n: i for i, n in enumerate(PIECES)}


def _mlp_order(l):
    o = []
    for f in range(4):
        o += [f"w1_{l}_{f}", f"w2_{l}_{f}"]
    return o


def build_program():
    nc = bass.Bass("TRN2", target_bir_lowering=False)
    dI = lambda n, s: nc.dram_tensor(n, s, F32, kind="ExternalInput").ap()
    dO = lambda n, s: nc.dram_tensor(n, s, F32, kind="ExternalOutput").ap()
    W = dI("W", [len(PIECES), 128, 8, 1024])
    xP = dI("xP", [128, 8, 1024])
    xS = dI("xS", [128, 8, 1056])
    xH = dI("xH", [128, 8, 512])
    cT = dI("cT", [128, 8, 2])
    vecs_d = dI("vecs", [128, NVEC])
    kctx_d = dI("kctx", [128, 4, 256])
    vctx_d = dI("vctx", [128, 2, 4, 64])
    rope_d = dI("rope", [128, 2, 1536])
    masks_d = dI("masks", [128, 4, 512])
    ident_d = dI("ident", [128, 128])
    yP = dO("yP", [128, 8, 1024])
    yS = dO("yS", [128, 8, 1024])
    kPo = dO("kP", [64, 4, 1024])
    vPo = dO("vP", [128, 8, 256])

    S = Sched(nc, 207 * 1024)
    op, dma = S.op, S.dma

    VEC = S.alloc([NVEC], F32)
    MODS = S.alloc([2, 48, 2], F32)
    COEF = S.alloc([4, 2, 5, 8], F32)
    CTMP = S.alloc([16], F32)
    EPSC = S.alloc([1], F32)
    CIN = S.alloc([8, 2], F32)
    SILUC = S.alloc([8, 2], BF16)
    ONES = S.alloc([128], BF16)
    MASKS = S.alloc([4, 512], BF16)
    IDENT = S.alloc([128], F32)
    ROPE = S.alloc([2, 1536], F32)
    RING = [S.alloc([8, 1024], BF16) for _ in range(4)]
    R = S.alloc([8, 1056], F32)
    H = S.alloc([8, 1056], BF16)
    XBASE = S.sb_ptr
    XEND = S.sbuf_bytes
    print("SBUF persistent bytes", XBASE, "scratch", XEND - XBASE)

    def vcol(name, i=0):
        o = VOFF[name] + i
        return VEC.ap[:, o:o + 1]

    def vrange(name, a, n):
        o = VOFF[name] + a
        return VEC.ap[:, o:o + n]

    bank_rr = {"mm": [0, [0, 1, 2, 3, 6, 7, 4, 5]], "st": [0, [4, 5]], "ln": [0, [4, 5, 6, 7]], "sc": [0, [6, 7, 0, 1, 2, 3, 4, 5]]}

    def bank(pool, width=512):
        st = bank_rr[pool]
        b = st[1][st[0] % len(st[1])]
        st[0] += 1
        return S.psum(b, width)

    ring_state = {"order": [], "next_load": 0, "slot_of": {}, "use": 0, "free": list(range(4))}

    def ring_extend(names):
        ring_state["order"].extend(names)

    def ring_issue():
        rs = ring_state
        i = rs["next_load"]
        name = rs["order"][i]
        si = rs["free"].pop(0)
        slot = RING[si]
        ncols = 256 if name == "wv" else (512 if name == "wkP" else 1024)
        p = PIDX["wk" if name == "wkP" else name]
        for hh in range(2):
            hv = V("sb", slot.lo + hh * 8192, slot.lo + (hh + 1) * 8192, None)
            dma("pool", slot.ap[:, hh * 4:(hh + 1) * 4, 0:ncols], W[p, :, hh * 4:(hh + 1) * 4, 0:ncols], writes=[hv])
        rs["slot_of"][i] = (slot, si)
        rs["next_load"] += 1

    def prefetch():
        rs = ring_state
        while rs["free"] and rs["next_load"] < len(rs["order"]):
            ring_issue()

    def need(name):
        rs = ring_state
        i = rs["use"]
        assert rs["order"][i] == name, (rs["order"][i], name)
        while rs["next_load"] <= i:
            assert rs["free"], ("ring full", name)
            ring_issue()
        rs["use"] += 1
        return rs["slot_of"][i][0]

    def release(slot):
        si = [k for k in range(4) if RING[k] is slot][0]
        assert si not in ring_state["free"]
        ring_state["free"].append(si)
        prefetch()

    dma("sp", VEC.ap, vecs_d, writes=[VEC])
    dma("sp", CIN.ap, cT, writes=[CIN])
    dma("sp", ROPE.ap, rope_d, writes=[ROPE])
    dma("pool", MASKS.ap, masks_d, writes=[MASKS])
    dma("sp", IDENT.ap, ident_d, writes=[IDENT])
    op("dve", lambda h: h.memset(EPSC.ap, EPS), writes=[EPSC])
    op("dve", lambda h: h.memset(ONES.ap, 1.0), writes=[ONES])
    op("act", lambda h: h.activation(SILUC.ap, CIN.ap, AF.Silu), reads=[CIN], writes=[SILUC])
    op("act", lambda h: h.activation(vrange("es", 0, 16), vrange("es", 0, 16), AF.Exp), reads=[VEC], writes=[VEC])

    def ada(l, blocks=range(6)):
        for j in blocks:
            slot = need(f"ada{l}_{j}")
            for m in range(8):
                ps = bank("mm", 2)
                for kc in range(8):
                    op("pe", lambda h, ps=ps, slot=slot, m=m, kc=kc: h.matmul(
                        ps.ap, slot.ap[:, kc, m * 128:(m + 1) * 128], SILUC.ap[:, kc, :], start=(kc == 0), stop=(kc == 7)),
                        reads=[slot, SILUC], writes=[ps], inc=(kc == 7))
                op("dve", lambda h, ps=ps, l=l, j=j, m=m: h.tensor_scalar(
                    MODS.ap[:, l, j * 8 + m, :], ps.ap, vcol(f"ada_b{l}", j * 8 + m), None, ALU.add),
                    reads=[ps, VEC], writes=[MODS])
            release(slot)

    def ada1_bufs():
        return [S.alloc([8, 512], BF16, at=XBASE + 43008 + i * 8192) for i in range(3)]

    def ada1_load(bufs, hp):
        j, half = hp // 2, hp % 2
        buf = bufs[hp % 3]
        dma("pool", buf.ap, W[PIDX[f"ada1_{j}"], :, :, half * 512:(half + 1) * 512], writes=[buf])

    def ada1_compute(bufs, hp):
        j, half = hp // 2, hp % 2
        buf = bufs[hp % 3]
        for m4 in range(4):
            m = half * 4 + m4
            ps = bank("mm", 2)
            for kc in range(8):
                op("pe", lambda h, ps=ps, m4=m4, kc=kc: h.matmul(
                    ps.ap, buf.ap[:, kc, m4 * 128:(m4 + 1) * 128], SILUC.ap[:, kc, :], start=(kc == 0), stop=(kc == 7)),
                    reads=[buf, SILUC], writes=[ps], inc=(kc == 7))
            op("dve", lambda h, ps=ps, j=j, m=m: h.tensor_scalar(
                MODS.ap[:, 1, j * 8 + m, :], ps.ap, vcol("ada_b1", j * 8 + m), None, ALU.add),
                reads=[ps, VEC], writes=[MODS])

    def mod(l, j, cond):
        return MODS.ap[:, l, j * 8:(j + 1) * 8, cond]

    def coef(sub, cond, l, jsh, jsc, jg, lng, lnb, bias, part="all"):
        C = lambda k: COEF.ap[:, sub, cond, k, :]
        t1 = CTMP.ap[:, 0:8]
        rd, wr = [MODS, VEC, COEF, CTMP], [COEF, CTMP]
        if part == "g":
            assert lng is None and bias is None
            op("dve", lambda h: h.tensor_copy(C(4), mod(l, jg, cond)), reads=rd, writes=wr)
            return
        op("dve", lambda h: h.tensor_scalar(t1, mod(l, jsc, cond), 1.0, None, ALU.add), reads=rd, writes=wr)
        if lng is None:
            op("dve", lambda h: h.tensor_copy(C(0), t1), reads=rd, writes=wr)
            op("dve", lambda h: h.tensor_copy(C(1), mod(l, jsh, cond)), reads=rd, writes=wr)
            op("dve", lambda h: h.memset(C(2), ALPHA), reads=rd, writes=wr)
            if bias is None:
                op("dve", lambda h: h.memset(C(3), 0.0), reads=rd, writes=wr)
            else:
                op("dve", lambda h: h.tensor_tensor(C(3), mod(l, jg, cond), vrange(bias, 0, 8), ALU.mult), reads=rd, writes=wr)
            if part == "h":
                return
        else:
            g, b = vrange(lng, 0, 8), vrange(lnb, 0, 8)
            op("dve", lambda h: h.tensor_tensor(C(0), t1, g, ALU.mult), reads=rd, writes=wr)
            op("dve", lambda h: h.tensor_tensor(C(1), t1, b, ALU.mult), reads=rd, writes=wr)
            op("dve", lambda h: h.tensor_tensor(C(1), C(1), mod(l, jsh, cond), ALU.add), reads=rd, writes=wr)
            op("dve", lambda h: h.tensor_scalar(C(2), g, ALPHA, None, ALU.mult), reads=rd, writes=wr)
            if bias is None:
                op("dve", lambda h: h.tensor_scalar(C(3), b, ALPHA, None, ALU.mult), reads=rd, writes=wr)
            else:
                op("dve", lambda h: h.tensor_tensor(C(3), mod(l, jg, cond), vrange(bias, 0, 8), ALU.mult), reads=rd, writes=wr)
                op("dve", lambda h: h.scalar_tensor_tensor(C(3), b, ALPHA, C(3), ALU.mult, ALU.add), reads=rd, writes=wr)
        op("dve", lambda h: h.tensor_copy(C(4), mod(l, jg, cond)), reads=rd, writes=wr)

    def cf(sub, cond, k, c):
        return COEF.ap[:, sub, cond, k, c:c + 1]

    def prep(sub, cond, a0, tw, src=None):
        for c in range(8):
            sv_ = R.v(c, a0, tw) if src is None else src.v(c, 0, tw)
            op("act", lambda h, c=c, sv_=sv_: h.activation(H.ap[:, c, a0:a0 + tw], sv_.ap, AF.Identity,
                                                           bias=cf(sub, cond, 1, c), scale=cf(sub, cond, 0, c)),
               reads=[sv_, COEF], writes=[H.v(c, a0, tw)])
        for c in range(8):
            sv_ = R.v(c, a0, tw) if src is None else src.v(c, 0, tw)
            if c % 4 != 3:
                op("dve", lambda h, c=c, sv_=sv_: h.tensor_scalar(R.ap[:, c, a0:a0 + tw], sv_.ap,
                                                                  cf(sub, cond, 2, c), cf(sub, cond, 3, c), ALU.mult, ALU.add),
                   reads=[sv_, COEF], writes=[R.v(c, a0, tw)])
            else:
                op("act", lambda h, c=c, sv_=sv_: h.activation(R.ap[:, c, a0:a0 + tw], sv_.ap, AF.Identity,
                                                               bias=cf(sub, cond, 3, c), scale=cf(sub, cond, 2, c)),
                   reads=[sv_, COEF], writes=[R.v(c, a0, tw)])

    def linear(slot, col0, nm, act, a0, tw, evac, nk=8, pool="mm", kbase=0):
        for m in range(nm):
            ps = bank(pool, tw)
            for kc in range(nk):
                op("pe", lambda h, ps=ps, m=m, kc=kc: h.matmul(
                    ps.ap, slot.ap[:, kbase + kc, col0 + m * 128: col0 + (m + 1) * 128], act.ap[:, kc, a0:a0 + tw],
                    start=(kc == 0), stop=(kc == nk - 1)),
                    reads=[slot, act.v(kc, a0, tw)], writes=[ps], inc=(kc == nk - 1))
            evac(m, ps)

    def ln_tmps(tw, tmp_at):
        ZB = S.alloc([8, tw], BF16, at=tmp_at)
        ZQ = S.alloc([8, tw], BF16, at=tmp_at + 8 * tw * 2)
        MEAN = S.alloc([tw], F32, at=tmp_at + 16 * tw * 2)
        RSTD = S.alloc([tw], F32, at=tmp_at + 16 * tw * 2 + tw * 4)
        return ZB, ZQ, MEAN, RSTD

    def ln_pre_chunk(src, c, a0, tw, tmp_at, copy_eng="act"):
        ZB, ZQ, _, _ = ln_tmps(tw, tmp_at)
        op("act", lambda h: h.activation(ZQ.ap[:, c, :], src.ap[:, c, a0:a0 + tw], AF.Square),
           reads=[src.v(c, a0, tw)], writes=[ZQ.v(c, 0, tw)])
        if copy_eng == "act":
            op("dve", lambda h: h.tensor_copy(ZB.ap[:, c, :], src.ap[:, c, a0:a0 + tw]),
               reads=[src.v(c, a0, tw)], writes=[ZB.v(c, 0, tw)])
        elif copy_eng == "actcopy":
            op("act", lambda h: h.activation(ZB.ap[:, c, :], src.ap[:, c, a0:a0 + tw], AF.Copy),
               reads=[src.v(c, a0, tw)], writes=[ZB.v(c, 0, tw)])
        else:
            op("pool", lambda h: h.tensor_copy(ZB.ap[:, c, :], src.ap[:, c, a0:a0 + tw]),
               reads=[src.v(c, a0, tw)], writes=[ZB.v(c, 0, tw)])

    def accum(sub, cond, a0, tw, ln_pre=False):
        def ev(m, ps):
            op("dve", lambda h: h.scalar_tensor_tensor(R.ap[:, m, a0:a0 + tw], ps.ap, cf(sub, cond, 4, m), R.ap[:, m, a0:a0 + tw],
                                                       ALU.mult, ALU.add),
               reads=[ps, COEF, R.v(m, a0, tw)], writes=[R.v(m, a0, tw)])
            if ln_pre:
                ln_pre_chunk(R, m, a0, tw, LN_TMP, copy_eng="actcopy")
        return ev

    def layer_norm(src, a0, tw, tmp_at, dst=None, copy_eng="act", pre_done=False):
        dst = src if dst is None else dst
        ZB, ZQ, MEAN, RSTD = ln_tmps(tw, tmp_at)
        if not pre_done:
            for c in range(8):
                ln_pre_chunk(src, c, a0, tw, tmp_at, copy_eng)
        p1, p2 = bank("ln", tw), bank("ln", tw)
        for c in range(8):
            op("pe", lambda h, c=c: h.matmul(p1.ap, ONES.ap, ZB.ap[:, c, :], start=(c == 0), stop=(c == 7)),
               reads=[ONES, ZB.v(c, 0, tw)], writes=[p1], inc=(c == 7))
        for c in range(8):
            op("pe", lambda h, c=c: h.matmul(p2.ap, ONES.ap, ZQ.ap[:, c, :], start=(c == 0), stop=(c == 7)),
               reads=[ONES, ZQ.v(c, 0, tw)], writes=[p2], inc=(c == 7))
        op("dve", lambda h: h.tensor_scalar(MEAN.ap, p1.ap, 1.0 / D, None, ALU.mult), reads=[p1], writes=[MEAN])
        op("dve", lambda h: h.tensor_tensor(RSTD.ap, MEAN.ap, MEAN.ap, ALU.mult), reads=[MEAN], writes=[RSTD])
        op("dve", lambda h: h.scalar_tensor_tensor(RSTD.ap, p2.ap, 1.0 / D, RSTD.ap, ALU.mult, ALU.subtract),
           reads=[p2, RSTD], writes=[RSTD])
        op("act", lambda h: h.activation(RSTD.ap, RSTD.ap, AF.Ln, bias=EPSC.ap[:, 0:1]), reads=[RSTD, EPSC], writes=[RSTD])
        op("act", lambda h: h.activation(RSTD.ap, RSTD.ap, AF.Exp, scale=-0.5), reads=[RSTD], writes=[RSTD])
        for c in range(8):
            op("dve", lambda h, c=c: h.tensor_tensor(dst.ap[:, c, a0:a0 + tw], src.ap[:, c, a0:a0 + tw], MEAN.ap, ALU.subtract),
               reads=[src.v(c, a0, tw), MEAN], writes=[dst.v(c, a0, tw)])
        for c in range(8):
            op("dve", lambda h, c=c: h.tensor_tensor(dst.ap[:, c, a0:a0 + tw], dst.ap[:, c, a0:a0 + tw], RSTD.ap, ALU.mult),
               reads=[dst.v(c, a0, tw), RSTD], writes=[dst.v(c, a0, tw)])

    def mlp(l, sub, cond, tiles, on_done=None, stage_hook=None):
        HID = [S.alloc([8, 512], BF16, at=XBASE + i * 8192) for i in range(2)]
        HT = [S.alloc([512], F32, at=XBASE + 36864 + i * 2048) for i in range(3)]
        ht_rr = [0]
        work = [(f, t) for f in range(4) for t in tiles]
        slots = {}

        def stage1(i):
            f, (a0, tw) = work[i]
            if stage_hook is not None:
                stage_hook(i)
            if f not in slots:
                slots[f] = (need(f"w1_{l}_{f}"), need(f"w2_{l}_{f}"))
            hid = HID[i % 2]

            def ev(m, ps):
                ht = HT[ht_rr[0] % 3]
                ht_rr[0] += 1
                op("act", lambda h: h.activation(ht.ap[:, 0:tw], ps.ap, AF.Relu, bias=vcol(f"b1_{l}", f * 8 + m)),
                   reads=[ps, VEC], writes=[ht])
                op("dve", lambda h: h.scalar_tensor_tensor(hid.ap[:, m, 0:tw], ps.ap, vcol(f"b1_{l}", f * 8 + m), ht.ap[:, 0:tw],
                                                           ALU.add, ALU.mult),
                   reads=[ps, VEC, ht], writes=[hid.v(m, 0, tw)])
            linear(slots[f][0], 0, 8, H, a0, tw, ev)
            if i + 1 == len(work) or work[i + 1][0] != f:
                release(slots[f][0])

        def stage2(i):
            f, (a0, tw) = work[i]
            hid = HID[i % 2]
            ev = accum(sub, cond, a0, tw, ln_pre=(f == 3))
            for m in range(8):
                ps = bank("mm", tw)
                for kc in range(8):
                    op("pe", lambda h, ps=ps, m=m, kc=kc: h.matmul(
                        ps.ap, slots[f][1].ap[:, kc, m * 128:(m + 1) * 128], hid.ap[:, kc, 0:tw], start=(kc == 0), stop=(kc == 7)),
                        reads=[slots[f][1], hid.v(kc, 0, tw)], writes=[ps], inc=(kc == 7))
                ev(m, ps)
            if i + 1 == len(work) or work[i + 1][0] != f:
                release(slots[f][1])
            if f == 3 and on_done is not None:
                on_done(work[i][1])

        for i in range(len(work)):
            stage1(i)
            if i > 0:
                stage2(i - 1)
        stage2(len(work) - 1)

    LN_TMP = XBASE + 16384

    def attn_bufs(nq, nkb, sphase):
        p = XBASE
        b = {}

        def A(name, shape, dt):
            nonlocal p
            b[name] = S.alloc(shape, dt, at=p)
            p = (p + int(np.prod(shape)) * DTSZ[dt] + 63) // 64 * 64
        A("Q", [8, nq], BF16)
        A("PT", [8, 512], BF16)
        A("REC", [1024], F32)
        A("KT", [4, nkb * 128], BF16)
        A("VA", [nkb, 4, 128], BF16)
        if sphase:
            A("RT", [4, 352], F32)
            A("KC", [4, 256], BF16)
            A("VC", [2, 4, 128], BF16)
        b["end"] = p
        return b

    pt_rr = [0]
    rt_rr = [0]
    rec_rr = [0]
    attn_pending = []
    GORD = [0, 2, 1, 3]

    scp_rr = [0]
    SC_PAIRS = [(0, 1), (2, 3), (6, 7)]

    def attn_core(B, j, qcol, nqw, ocol, ktiles):
        po = bank("st", 4 * nqw)
        nk = len(ktiles)
        ptvs = []

        def scores(ti):
            kfn, va_ap, va_v, mask = ktiles[ti]
            pi = pt_rr[0] % 8
            pt_rr[0] += 1
            ptv = B["PT"].v(pi, 0, 4 * nqw)
            ptvs.append(ptv)
            b0, b1 = SC_PAIRS[scp_rr[0] % 3]
            scp_rr[0] += 1
            pss = [S.psum(b0, 2 * nqw), S.psum(b1, 2 * nqw)]
            for half in range(2):
                ps = pss[half]
                k_ap, k_v = kfn(half)
                for gi in range(2):
                    hd = 4 * j + GORD[half * 2 + gi]
                    c = hd // 2
                    assert hd % 2 == half
                    op("pe", lambda h, ps=ps, gi=gi, c=c, half=half, k_ap=k_ap: h.matmul(
                        ps.ap[:, gi * nqw:(gi + 1) * nqw], k_ap, B["Q"].ap[half * 64:(half + 1) * 64, c, qcol:qcol + nqw],
                        start=True, stop=True),
                        reads=[k_v, B["Q"].v(c, qcol, nqw)], writes=[ps], inc=(gi == 1))
            src = S.ps[:, b0 * 512:(b0 + 2) * 512].rearrange("p (b n) -> p b n", b=2)[:, :, 0:2 * nqw]
            dst = ptv.ap.rearrange("p (b n) -> p b n", b=2)
            op("act", lambda h, src=src, dst=dst: h.activation(dst, src, AF.Exp, scale=SCALE), reads=pss, writes=[ptv])
            if mask is not None:
                mv, mq0 = mask
                m_ap = MASKS.ap[:, mv, :].rearrange("p (g q) -> p g q", g=4)[:, :, mq0:mq0 + nqw]
                p_ap = ptv.ap.rearrange("p (g q) -> p g q", g=4)
                op("dve", lambda h, p_ap=p_ap, m_ap=m_ap: h.tensor_tensor(p_ap, p_ap, m_ap, ALU.mult),
                   reads=[ptv, MASKS], writes=[ptv])

        def pv(ti):
            kfn, va_ap, va_v, mask = ktiles[ti]
            ptv = ptvs[ti]
            op("pe", lambda h, ptv=ptv, va_ap=va_ap, ti=ti: h.matmul(po.ap, va_ap, ptv.ap, start=(ti == 0), stop=(ti == nk - 1)),
               reads=[va_v, ptv], writes=[po], inc=(ti == nk - 1))

        for ti in range(min(3, nk)):
            scores(ti)
        for ti in range(nk):
            pv(ti)
            if ti + 3 < nk:
                scores(ti + 3)
        attn_pending.append(lambda: attn_norm(B, j, nqw, ocol, po))
        if len(attn_pending) > 1:
            attn_pending.pop(0)()

    def attn_flush():
        while attn_pending:
            attn_pending.pop(0)()

    def attn_norm(B, j, nqw, ocol, po):
        rec = B["REC"].cols((rec_rr[0] % 2) * 512, 4 * nqw)
        rec_rr[0] += 1
        eo = VOFF["es"] + 4 * j
        es_b = VEC.ap[64:128, eo:eo + 4].unsqueeze(2).to_broadcast([64, 4, nqw])
        r3 = rec.ap[64:128, :].rearrange("p (g q) -> p g q", g=4)
        op("dve", lambda h: h.tensor_tensor(r3, po.ap[64:128, :].rearrange("p (g q) -> p g q", g=4), es_b, ALU.add),
           reads=[po, VEC], writes=[rec])
        op("act", lambda h: h.activation(rec.ap[64:128, :], rec.ap[64:128, :], AF.Ln), reads=[rec], writes=[rec])
        op("act", lambda h: h.activation(rec.ap[64:128, :], rec.ap[64:128, :], AF.Exp, scale=-1.0), reads=[rec], writes=[rec])
        for half in range(2):
            sl = slice(half * 2 * nqw, (half + 1) * 2 * nqw)
            op("dve", lambda h, half=half, sl=sl: h.tensor_tensor(
                H.ap[half * 64:(half + 1) * 64, 2 * j:2 * j + 2, ocol:ocol + nqw],
                po.ap[0:64, sl].rearrange("p (g q) -> p g q", g=2),
                rec.ap[64:128, sl].rearrange("p (g q) -> p g q", g=2), ALU.mult),
                reads=[po, rec], writes=[H.v(2 * j, ocol, nqw), H.v(2 * j + 1, ocol, nqw)])

    def v_block(B, slot, src, scol, blk, kout=None):
        ps = bank("mm", 256)
        for kc in range(8):
            op("pe", lambda h, kc=kc: h.matmul(ps.ap, src.ap[:, kc, scol:scol + 128], slot.ap[:, kc, 0:256], start=(kc == 0), stop=(kc == 7)),
               reads=[slot, src.v(kc, scol, 128)], writes=[ps], inc=(kc == 7))
        va = B["VA"]
        lo = va.lo + blk * 4 * 128 * 2
        vav = V("sb", lo, lo + 4 * 128 * 2, va.ap[:, blk, :, 0:64])
        op("act", lambda h: h.activation(vav.ap, ps.ap.rearrange("p (j d) -> p j d", j=4), AF.Copy), reads=[ps], writes=[vav])
        if kout is not None:
            op("dve", lambda h: h.tensor_copy(kout.ap[:, blk, :], ps.ap), reads=[ps], writes=[kout.v(blk, 0, 256)])

    def va_view(B, blk, j):
        va = B["VA"]
        lo = va.lo + (blk * 4 + j) * 128 * 2
        return va.ap[:, blk, j, :], V("sb", lo, lo + 256, None)

    ring_extend(["ada0_0", "ada0_1", "wq", "wkP", "wv"] + [f"ada0_{j}" for j in range(2, 6)] + ["wo"] + _mlp_order(0) + ["pw1a", "pw1g", "pw2"] + _mlp_order(1))
    ring_extend(["wk", "wv", "wq", "wqs", "wo"] + _mlp_order(0) + ["pw1a", "pw1g", "pw2"] + _mlp_order(1))
    prefetch()
    ada(0, [0, 1])

    def conv_sublayer(cond, T_in, in_tiles, segs, out_tiles, ucols):
        ACC = S.alloc([8, 1024], F32, at=XBASE)
        lnp = XBASE + 32768
        p = XBASE + 43008
        UP = S.alloc([8, ucols], BF16, at=p)
        p = (p + 8 * ucols * 2 + 63) // 64 * 64
        SG = [S.alloc([512], F32, at=p + i * 2048) for i in range(2)]
        p += 4096
        DG = S.alloc([16, 128], BF16, at=p)
        p += 4096
        assert p <= XEND, (p, XEND)
        sa, sg_ = need("pw1a"), need("pw1g")
        op("dve", lambda h: h.memset(UP.ap, 0.0), writes=[UP])
        def upcol(col):
            for (c0, ln, u0) in segs:
                if c0 <= col < c0 + ln:
                    return u0 + (col - c0)
            raise AssertionError(col)
        sgi = [0]
        for (a0, tw) in in_tiles:
            parts = []
            x0 = a0
            while x0 < a0 + tw:
                for (c0, ln, u0) in segs:
                    if c0 <= x0 < c0 + ln:
                        e = min(a0 + tw, c0 + ln)
                        parts.append((x0, e - x0, u0 + x0 - c0))
                        x0 = e
                        break
                else:
                    raise AssertionError
            for m in range(8):
                psg = bank("mm", tw)
                for kc in range(8):
                    op("pe", lambda h, psg=psg, m=m, kc=kc: h.matmul(psg.ap, sg_.ap[:, kc, m * 128:(m + 1) * 128], H.ap[:, kc, a0:a0 + tw],
                                                                     start=(kc == 0), stop=(kc == 7)),
                       reads=[sg_, H.v(kc, a0, tw)], writes=[psg], inc=(kc == 7))
                sg = SG[sgi[0] % 2]
                sgi[0] += 1
                op("act", lambda h, psg=psg, sg=sg, m=m: h.activation(sg.ap[:, 0:tw], psg.ap, AF.Sigmoid, bias=vcol("pw1_b", 8 + m)),
                   reads=[psg, VEC], writes=[sg])
                psa = bank("mm", tw)
                for kc in range(8):
                    op("pe", lambda h, psa=psa, m=m, kc=kc: h.matmul(psa.ap, sa.ap[:, kc, m * 128:(m + 1) * 128], H.ap[:, kc, a0:a0 + tw],
                                                                     start=(kc == 0), stop=(kc == 7)),
                       reads=[sa, H.v(kc, a0, tw)], writes=[psa], inc=(kc == 7))
                for (x0, ln, u0) in parts:
                    op("dve", lambda h, psa=psa, sg=sg, m=m, x0=x0, ln=ln, u0=u0: h.scalar_tensor_tensor(
                        UP.ap[:, m, u0:u0 + ln], psa.ap[:, x0 - a0:x0 - a0 + ln], vcol("pw1_b", m), sg.ap[:, x0 - a0:x0 - a0 + ln],
                        ALU.add, ALU.mult),
                        reads=[psa, sg, VEC], writes=[UP.v(m, u0, ln)])
        release(sa); release(sg_)
        return UP, ACC, lnp, DG

    dg_rr = [0]

    def conv_taps(UP, ACC, lnp, DG, out_specs):
        for c in range(8):
            pss = [bank("sc", tw) for (_, tw, _) in out_specs]
            for w in range(31):
                di = dg_rr[0] % 16
                dg_rr[0] += 1
                dv = DG.v(di, 0, 128)
                if w % 2 == 0:
                    op("act", lambda h, dv=dv, w=w, c=c: h.activation(dv.ap, IDENT.ap, AF.Identity, scale=vcol("dw_w", w * 8 + c)),
                       reads=[IDENT, VEC], writes=[dv])
                else:
                    op("dve", lambda h, dv=dv, w=w, c=c: h.tensor_scalar(dv.ap, IDENT.ap, vcol("dw_w", w * 8 + c), None, ALU.mult),
                       reads=[IDENT, VEC], writes=[dv])
                for oi, (ps, (uc0, tw, d0)) in enumerate(zip(pss, out_specs)):
                    op("pe", lambda h, ps=ps, dv=dv, c=c, w=w, uc0=uc0, tw=tw: h.matmul(
                        ps.ap, dv.ap, UP.ap[:, c, uc0 - 15 + w:uc0 - 15 + w + tw], start=(w == 0), stop=(w == 30)),
                        reads=[dv, UP.v(c, uc0 - 15 + w, tw)], writes=[ps], inc=(w == 30 or oi == len(pss) - 1))
            for ps, (uc0, tw, d0) in zip(pss, out_specs):
                op("dve", lambda h, ps=ps, c=c, d0=d0, tw=tw: h.tensor_scalar(ACC.ap[:, c, d0:d0 + tw], ps.ap, vcol("dw_b", c), None, ALU.add),
                   reads=[ps, VEC], writes=[ACC.v(c, d0, tw)])

    def conv_post(ACC, lnp, own0):
        for i in range(4):
            d0 = i * 256
            layer_norm(ACC, d0, 256, lnp, copy_eng="pool")
            for c in range(8):
                op("act", lambda h, c=c, d0=d0: h.activation(H.ap[:, c, own0 + d0:own0 + d0 + 256], ACC.ap[:, c, d0:d0 + 256], AF.Silu,
                                                             bias=vcol("cn_b", c), scale=vcol("cn_g", c)),
                   reads=[ACC.v(c, d0, 256), VEC], writes=[H.v(c, own0 + d0, 256)])

    XSTAGE = [S.alloc([8, 352], F32, at=XBASE + 43008 + i * 11264) for i in range(2)]

    def run_phase(ph):
        cond = ph
        if ph == 0:
            T_, tiles, own0 = 1024, [(0, 512), (512, 512)], 0
            xin = xP
        else:
            T_, tiles, own0 = 1056, [(0, 352), (352, 352), (704, 352)], 16
            xin = xS
        own_tiles = [(own0, 512), (own0 + 512, 512)]
        xsrc = {}
        for ti, (a0, tw) in enumerate(tiles):
            if ph == 1 and ti < 2:
                xsrc[a0] = XSTAGE[ti]
            else:
                dma("sp", R.ap[:, :, a0:a0 + tw], xin[:, :, a0:a0 + tw], writes=R.vs(a0, tw))
        coef(0, cond, 0, 0, 1, 2, None, None, None, part=("h" if ph == 0 else "all"))
        B = attn_bufs(T_, 8 if ph == 0 else 12, ph == 1)
        if ph == 1:
            for (a0, tw) in tiles:
                prep(0, cond, a0, tw, src=xsrc.get(a0))
        op("pool", lambda h: h.memset(B["VA"].ap[:, :, :, 64:128], 1.0), writes=[B["VA"]])
        if ph == 1:
            op("pool", lambda h: h.memset(B["VC"].ap[:, :, :, 64:128], 1.0), writes=[B["VC"]])
        if ph == 0:
            KOUT = S.alloc([4, 1024], F32, at=B["end"])
            VOUT = S.alloc([8, 256], F32, at=B["end"] + 16384)
            assert B["end"] + 16384 + 8192 <= XEND
            for (a0, tw) in tiles:
                prep(0, cond, a0, tw)
            sq, sk, sv = need("wq"), need("wkP"), need("wv")
            for (a0, tw) in tiles:
                def evq(m, ps, a0=a0, tw=tw):
                    op("act", lambda h: h.activation(B["Q"].ap[:, m, a0:a0 + tw], ps.ap, AF.Copy), reads=[ps], writes=[B["Q"].v(m, a0, tw)])
                linear(sq, 0, 8, H, a0, tw, evq)

                def evk(m, ps, a0=a0, tw=tw):
                    op("act", lambda h: h.activation(B["KT"].ap[:, m, a0:a0 + tw], ps.ap, AF.Copy), reads=[ps], writes=[B["KT"].v(m, a0, tw)])
                    op("dve", lambda h: h.tensor_copy(KOUT.ap[0:64, m, a0:a0 + tw], ps.ap[0:64, :]), reads=[ps], writes=[KOUT.v(m, a0, tw)])
                linear(sk, 0, 4, H, a0, tw, evk)
                for blk in range(a0 // 128, (a0 + tw) // 128):
                    v_block(B, sv, H, blk * 128, blk, kout=VOUT)
            release(sq); release(sk); release(sv)
            dma("sp", kPo, KOUT.ap[0:64, :, :], reads=[KOUT])
            dma("sp", vPo, VOUT.ap, reads=[VOUT])
            for s in range(4):
                ada(0, [2 + s])
                for j in range(4):
                    kts = []
                    for kt in range(2):
                        col = s * 256 + kt * 128
                        kfn = (lambda half, col=col, j=j: (B["KT"].ap[half * 64:(half + 1) * 64, j, col:col + 128], B["KT"].v(j, col, 128)))
                        va_ap, va_v = va_view(B, s * 2 + kt, j)
                        kts.append((kfn, va_ap, va_v, None))
                    for qh in range(2):
                        attn_core(B, j, s * 256 + qh * 128, 128, s * 256 + qh * 128, kts)
            coef(0, cond, 0, 0, 1, 2, None, None, None, part="g")
        else:
            XHT = S.alloc([8, 512], F32, at=XBASE)
            HH = S.alloc([8, 512], BF16, at=XBASE + 16384)
            dma("sp", XHT.ap, xH, writes=[XHT])
            for c in range(8):
                op("act", lambda h, c=c: h.activation(HH.ap[:, c, :], XHT.ap[:, c, :], AF.Identity, bias=cf(0, cond, 1, c), scale=cf(0, cond, 0, c)),
                   reads=[XHT.v(c, 0, 512), COEF], writes=[HH.v(c, 0, 512)])
            sk, sv = need("wk"), need("wv")
            dma("pool", B["KC"].ap, kctx_d, writes=[B["KC"]])
            dma("pool", B["VC"].ap[:, :, :, 0:64], vctx_d, writes=[B["VC"]])
            RT = B["RT"]

            def rope_evac(dst, m, ps_a, ps_b, dcol, tw, rcol):
                rb = (rt_rr[0] % 2) * 2
                rt_rr[0] += 1
                t1, t2 = RT.v(rb, 0, tw), RT.v(rb + 1, 0, tw)
                op("dve", lambda h: h.tensor_tensor(t1.ap, ps_a.ap, ROPE.ap[:, 0, rcol:rcol + tw], ALU.mult), reads=[ps_a, ROPE], writes=[t1])
                op("dve", lambda h: h.tensor_tensor(t2.ap, ps_b.ap, ROPE.ap[:, 1, rcol:rcol + tw], ALU.mult), reads=[ps_b, ROPE], writes=[t2])
                op("pool", lambda h: h.tensor_tensor(dst.ap[:, m, dcol:dcol + tw], t1.ap, t2.ap, ALU.add), reads=[t1, t2], writes=[dst.v(m, dcol, tw)])

            def k_proj(src, scol, tw, ecol):
                for m in range(4):
                    pa, pb = bank("mm", tw), bank("mm", tw)
                    for (pp, cbase) in ((pa, 0), (pb, 512)):
                        for kc in range(8):
                            op("pe", lambda h, pp=pp, cbase=cbase, m=m, kc=kc: h.matmul(
                                pp.ap, sk.ap[:, kc, cbase + m * 128:cbase + (m + 1) * 128], src.ap[:, kc, scol:scol + tw],
                                start=(kc == 0), stop=(kc == 7)),
                                reads=[sk, src.v(kc, scol, tw)], writes=[pp], inc=(kc == 7))
                    rope_evac(B["KT"], m, pa, pb, ecol, tw, ecol)

            k_proj(HH, 0, 256, 0)
            k_proj(HH, 256, 256, 1280)
            for i, blk in enumerate((0, 1, 10, 11)):
                v_block(B, sv, HH, i * 128, blk)
            for (a0, tw) in tiles:
                k_proj(H, a0, tw, 240 + a0)
            for blk in range(2, 10):
                v_block(B, sv, H, 16 + (blk - 2) * 128, blk)
            sq, sqs = need("wq"), need("wqs")
            for (a0, tw) in tiles:
                for m in range(8):
                    pa, pb = bank("mm", tw), bank("mm", tw)
                    for (pp, sl) in ((pa, sq), (pb, sqs)):
                        for kc in range(8):
                            op("pe", lambda h, pp=pp, sl=sl, m=m, kc=kc: h.matmul(
                                pp.ap, sl.ap[:, kc, m * 128:(m + 1) * 128], H.ap[:, kc, a0:a0 + tw], start=(kc == 0), stop=(kc == 7)),
                                reads=[sl, H.v(kc, a0, tw)], writes=[pp], inc=(kc == 7))
                    rope_evac(B["Q"], m, pa, pb, a0, tw, 240 + a0)
            release(sk); release(sv); release(sq); release(sqs)
            def ctx_tiles(j):
                out = []
                for kt in range(2):
                    kfn = (lambda half, kt=kt, j=j: (B["KC"].ap[half * 64:(half + 1) * 64, j, kt * 128:(kt + 1) * 128], B["KC"].v(j, kt * 128, 128)))
                    vc = B["VC"]
                    lo = vc.lo + (kt * 4 + j) * 256
                    out.append((kfn, vc.ap[:, kt, j, :], V("sb", lo, lo + 256, None), None))
                return out

            def loc_tile(e, j, mask):
                kfn = (lambda half, e=e, j=j: (B["KT"].ap[half * 64:(half + 1) * 64, j, e * 128:(e + 1) * 128], B["KT"].v(j, e * 128, 128)))
                va_ap, va_v = va_view(B, e, j)
                return (kfn, va_ap, va_v, mask)

            qblocks = [(1, 0, 16, 112)] + [(n + 2, 16 + n * 128, 128, 0) for n in range(8)] + [(10, 1040, 16, 0)]
            for (e, qcol, nqw, mq0) in qblocks:
                for j in range(4):
                    mprev = 2 if e == 2 else 0
                    mnext = 3 if e == 9 else 1
                    kts = [loc_tile(e - 1, j, (mprev, mq0)), loc_tile(e, j, None), loc_tile(e + 1, j, (mnext, mq0))] + ctx_tiles(j)
                    attn_core(B, j, qcol, nqw, qcol, kts)
        attn_flush()
        def fin(sub_next, t):
            layer_norm(R, t[0], t[1], LN_TMP, pre_done=True)
            prep(sub_next, cond, t[0], t[1])

        coef(1, cond, 0, 3, 4, 5, "ln1_g0", "ln1_b0", "b2_0")
        so = need("wo")
        for i, (a0, tw) in enumerate(tiles):
            linear(so, 0, 8, H, a0, tw, accum(0, cond, a0, tw, ln_pre=True))
            if i == len(tiles) - 1:
                release(so)
            fin(1, tiles[i])
        def coefs_l1():
            coef(2, cond, 1, 0, 1, 2, "ln2_g0", "ln2_b0", "pw2_b")
            coef(3, cond, 1, 3, 4, 5, "ln1_g1", "ln1_b1", "b2_1")

        if ph == 0:
            abufs = ada1_bufs()

            def hook(i):
                if 1 <= i <= 6:
                    ada1_compute(abufs, 2 * (i - 1))
                    ada1_compute(abufs, 2 * (i - 1) + 1)
                if i <= 5:
                    ada1_load(abufs, 2 * i)
                    ada1_load(abufs, 2 * i + 1)
                if i == 7:
                    coefs_l1()
            assert len(tiles) * 4 == 8
            mlp(0, 1, cond, tiles, on_done=lambda t: fin(2, t), stage_hook=hook)
        else:
            coefs_l1()
            mlp(0, 1, cond, tiles, on_done=lambda t: fin(2, t))
        if ph == 0:
            segs = [(s * 256, 256, s * 286 + 15) for s in range(4)]
            ucols = 4 * 286
        else:
            segs = [(0, 1056, 0)]
            ucols = 1056
        UP, ACC, lnp, DG = conv_sublayer(cond, T_, tiles, segs, None, ucols)
        if ph == 1:
            for c in range(8):
                op("dve", lambda h, c=c: h.tensor_scalar(UP.ap[:, c, 0:16], UP.ap[:, c, 0:16], vcol("valid", 0), None, ALU.mult),
                   reads=[UP.v(c, 0, 16), VEC], writes=[UP.v(c, 0, 16)])
                op("dve", lambda h, c=c: h.tensor_scalar(UP.ap[:, c, 1040:1056], UP.ap[:, c, 1040:1056], vcol("valid", 1), None, ALU.mult),
                   reads=[UP.v(c, 1040, 16), VEC], writes=[UP.v(c, 1040, 16)])
            out_specs = [(16 + i * 512, 512, i * 512) for i in range(2)]
        else:
            out_specs = [(s * 286 + 15, 256, s * 256) for s in range(4)]
        conv_taps(UP, ACC, lnp, DG, out_specs)
        conv_post(ACC, lnp, own0)
        if ph == 0:
            for ti in range(2):
                dma("sp", XSTAGE[ti].ap, xS[:, :, ti * 352:(ti + 1) * 352], writes=[XSTAGE[ti]])
        sp2 = need("pw2")
        for i, (a0, tw) in enumerate(own_tiles):
            linear(sp2, 0, 8, H, a0, tw, accum(2, cond, a0, tw, ln_pre=True))
            if i == len(own_tiles) - 1:
                release(sp2)
            fin(3, own_tiles[i])
        yout = yP if ph == 0 else yS

        def final(t):
            a0, tw = t
            layer_norm(R, a0, tw, LN_TMP, pre_done=True)
            for c in range(8):
                op("act", lambda h, c=c: h.activation(R.ap[:, c, a0:a0 + tw], R.ap[:, c, a0:a0 + tw], AF.Identity,
                                                      bias=vcol("ln2_b1", c), scale=vcol("ln2_g1", c)),
                   reads=[R.v(c, a0, tw), VEC], writes=[R.v(c, a0, tw)])
                if c % 2 == 1:
                    c0 = c - 1
                    dma("sp", yout[:, c0:c0 + 2, a0 - own0:a0 - own0 + tw], R.ap[:, c0:c0 + 2, a0:a0 + tw],
                        reads=[R.v(cc, a0, tw) for cc in range(c0, c0 + 2)])

        mlp(1, 3, cond, own_tiles, on_done=final)

    run_phase(0)
    run_phase(1)
    S.finish("sp")
    S.emit()
    S.close()
    return nc


def _fm(v):
    v = np.asarray(v, np.float32).reshape(-1, 128)
    return np.ascontiguousarray(v.T)


def _wpiece(w):
    out = np.zeros((128, 8, 1024), np.float32)
    out[:, :, :w.shape[1]] = w.reshape(8, 128, w.shape[1]).transpose(1, 0, 2)
    return out


_NC_CACHE = {}


def kernel(x_prompt, x_sample, cache_k, cache_v, c, c_ctx, ada_w, ada_b, attn_w_qkv, attn_w_o, attn_sink,
           conv_pw1_w, conv_pw1_b, conv_dw_w, conv_dw_b, conv_norm_g, conv_norm_b, conv_pw2_w, conv_pw2_b,
           ln1_g, ln1_b, mlp_w1, mlp_b1, mlp_w2, mlp_b2, ln2_g, ln2_b):
    f = lambda a: np.asarray(a, np.float32)
    x_prompt, x_sample, cache_k, cache_v, c, c_ctx = map(f, (x_prompt, x_sample, cache_k, cache_v, c, c_ctx))
    ada_w, ada_b, wqkv, wo = f(ada_w), f(ada_b), f(attn_w_qkv)[0], f(attn_w_o)[0]
    Wall = np.zeros((len(PIECES), 128, 8, 1024), np.float32)
    for l in range(2):
        for j in range(6):
            Wall[PIDX[f"ada{l}_{j}"]] = _wpiece(ada_w[l][:, j * 1024:(j + 1) * 1024])
        for fb in range(4):
            Wall[PIDX[f"w1_{l}_{fb}"]] = _wpiece(f(mlp_w1)[l][:, fb * 1024:(fb + 1) * 1024])
            Wall[PIDX[f"w2_{l}_{fb}"]] = _wpiece(f(mlp_w2)[l][fb * 1024:(fb + 1) * 1024, :])
    partner = np.concatenate([np.arange(16, 32), np.arange(0, 16), np.arange(48, 64), np.arange(32, 48)])
    wq = wqkv[:, 0:1024]
    wk = wqkv[:, 1024:1280]
    wv = wqkv[:, 1280:1536]
    qperm = (np.arange(16)[:, None] * 64 + partner[None, :]).reshape(-1)
    Wall[PIDX["wq"]] = _wpiece(wq)
    Wall[PIDX["wqs"]] = _wpiece(wq[:, qperm])
    kd = np.concatenate([np.concatenate([wk[:, j * 64:(j + 1) * 64]] * 2, 1) for j in range(4)], 1)
    ks = np.concatenate([np.concatenate([wk[:, j * 64 + partner]] * 2, 1) for j in range(4)], 1)
    Wall[PIDX["wk"]] = _wpiece(np.concatenate([kd, ks], 1))
    Wall[PIDX["wv"]] = _wpiece(wv)
    Wall[PIDX["wo"]] = _wpiece(wo)
    pw1 = f(conv_pw1_w)[0]
    Wall[PIDX["pw1a"]] = _wpiece(pw1[:, 0:1024])
    Wall[PIDX["pw1g"]] = _wpiece(pw1[:, 1024:2048])
    Wall[PIDX["pw2"]] = _wpiece(f(conv_pw2_w)[0])
    vec = np.zeros((128, NVEC), np.float32)

    def put(name, arr):
        arr = np.asarray(arr, np.float32)
        vec[:, VOFF[name]:VOFF[name] + arr.shape[1]] = arr
    for l in range(2):
        put(f"ada_b{l}", _fm(ada_b[l]))
        put(f"ln1_g{l}", _fm(f(ln1_g)[l])); put(f"ln1_b{l}", _fm(f(ln1_b)[l]))
        put(f"ln2_g{l}", _fm(f(ln2_g)[l])); put(f"ln2_b{l}", _fm(f(ln2_b)[l]))
        put(f"b1_{l}", _fm(f(mlp_b1)[l])); put(f"b2_{l}", _fm(f(mlp_b2)[l]))
    put("pw1_b", _fm(f(conv_pw1_b)[0]))
    put("dw_w", _fm(f(conv_dw_w)[0].reshape(-1)))
    put("dw_b", _fm(f(conv_dw_b)[0])); put("cn_g", _fm(f(conv_norm_g)[0])); put("cn_b", _fm(f(conv_norm_b)[0]))
    put("pw2_b", _fm(f(conv_pw2_b)[0]))
    gperm = np.array([4 * j + g for j in range(4) for g in (0, 2, 1, 3)])
    put("es", np.broadcast_to(f(attn_sink)[0][gperm][None, :], (128, 16)))
    half = 32
    inv = (10000.0 ** (-np.arange(0, half, 2, dtype=np.float32) / half)).astype(np.float32)
    ki = np.arange(128)[:, None]
    qi = np.arange(128)[None, :]
    tri_prev = (ki >= qi).astype(np.float32)
    tri_next = (ki <= qi).astype(np.float32)
    in_maps = []
    for core in range(8):
        b, s0 = core // 4, (core % 4) * 1024
        xp = x_prompt[4 * core:4 * core + 4].reshape(1024, 8, 128).transpose(2, 1, 0)

        def tok(lo, hi):
            out = np.zeros((hi - lo, 1024), np.float32)
            a, e = max(lo, 0), min(hi, 4096)
            out[a - lo:e - lo] = x_sample[b, a:e]
            return out
        xs = tok(s0 - 16, s0 + 1040).reshape(1056, 8, 128).transpose(2, 1, 0)
        xh = np.concatenate([tok(s0 - 256, s0), tok(s0 + 1024, s0 + 1280)], 0).reshape(512, 8, 128).transpose(2, 1, 0)
        ct = np.stack([c_ctx, c[b]], -1).reshape(8, 128, 2).transpose(1, 0, 2)
        v = vec.copy()
        vl, vr = float(s0 > 0), float(s0 + 1024 < 4096)
        v[:, VOFF["valid"]] = vl
        v[:, VOFF["valid"] + 1] = vr
        kc_ = cache_k[b, 0].transpose(2, 1, 0)
        kctx = np.concatenate([kc_, kc_], 0)
        vctx = cache_v[b, 0].reshape(2, 128, 4, 64).transpose(1, 0, 2, 3)
        pos = np.arange(s0 - 256, s0 + 1280)
        row = (pos // 64).astype(np.float32)
        col = (pos % 64).astype(np.float32)
        ang = np.concatenate([row[None, :] * inv[:, None]] * 2 + [col[None, :] * inv[:, None]] * 2, 0)
        cos = np.cos(ang).astype(np.float32)
        sin = np.sin(ang).astype(np.float32)
        sgn = np.concatenate([-np.ones(16), np.ones(16), -np.ones(16), np.ones(16)]).astype(np.float32)[:, None]
        rope = np.stack([np.concatenate([cos, cos], 0), np.concatenate([sin * sgn, sin * sgn], 0)], 1)
        mk = np.stack([np.tile(tri_prev, (1, 4)), np.tile(tri_next, (1, 4)),
                       np.tile(tri_prev, (1, 4)) * vl, np.tile(tri_next, (1, 4)) * vr], 1)
        in_maps.append({"W": Wall, "xP": np.ascontiguousarray(xp), "xS": np.ascontiguousarray(xs), "xH": np.ascontiguousarray(xh),
                        "cT": np.ascontiguousarray(ct), "vecs": v, "kctx": np.ascontiguousarray(kctx),
                        "vctx": np.ascontiguousarray(vctx), "rope": np.ascontiguousarray(rope.astype(np.float32)),
                        "masks": np.ascontiguousarray(mk.astype(np.float32)), "ident": np.eye(128, dtype=np.float32)})
    if "nc" not in _NC_CACHE:
        _NC_CACHE["nc"] = build_program()
    nc = _NC_CACHE["nc"]
    res = run_bass_kernel_spmd(nc, in_maps, core_ids=list(range(8)))
    y_prompt = np.zeros((32, 256, 1024), np.float32)
    y_sample = np.zeros((2, 4096, 1024), np.float32)
    new_k = np.zeros((32, 1, 256, 4, 64), np.float32)
    new_v = np.zeros((32, 1, 256, 4, 64), np.float32)
    for core in range(8):
        r = res.results[core]
        b, s0 = core // 4, (core % 4) * 1024
        y_prompt[4 * core:4 * core + 4] = r["yP"].transpose(2, 1, 0).reshape(4, 256, 1024)
        y_sample[b, s0:s0 + 1024] = r["yS"].transpose(2, 1, 0).reshape(1024, 1024)
        new_k[4 * core:4 * core + 4, 0] = r["kP"].transpose(2, 1, 0).reshape(4, 256, 4, 64)
        new_v[4 * core:4 * core + 4, 0] = r["vP"].transpose(1, 0, 2).reshape(4, 256, 4, 64)
    return (y_prompt, y_sample, new_k, new_v)
```
